# Optimizing a Trainium2 kernel written in Bass

```python
import math
import jax, jax.numpy as jnp
from jax import lax
import numpy as np

D_MODEL = 2048
BATCH = 8
SEQ = 2048
DEPTH = 2
DEC_BATCH = 8
DEC_SEQ = 64
PAST_LEN = 2048

CHUNK = 64
N_MIXERS = 2
N_ATTN_LAYERS = (DEPTH + 1) // 2
N_POOL_LAYERS = DEPTH // 2
HEAD_DIM = 64
N_HEADS = D_MODEL // HEAD_DIM
N_KV_HEADS = N_HEADS // 8
GQA_GROUP = N_HEADS // N_KV_HEADS
Q_DIM = N_HEADS * HEAD_DIM
KV_DIM = N_KV_HEADS * HEAD_DIM
QKV_DIM = Q_DIM + 2 * KV_DIM
WINDOW = 128
N_WIN_CHUNKS = WINDOW // CHUNK
ROT_DIM = HEAD_DIM // 4
ROPE_THETA = 500000.0
POOL_WINDOWS = (2, 4, 8, 16)
N_POOL_GROUPS = len(POOL_WINDOWS)
POOL_GROUP_DIM = D_MODEL // N_POOL_GROUPS
POOL_HIST = max(POOL_WINDOWS) - 1
D_FF = 4 * D_MODEL
PLE_DIM = 256
EPS = 1e-6

kernel_name = "chunk_stream_swa_sink_pool_hybrid"


def rms_norm(x, g):
    xf = x.astype(jnp.float32)
    y = xf * lax.rsqrt(jnp.mean(xf * xf, axis=-1, keepdims=True) + EPS)
    return (y * g.astype(jnp.float32)).astype(x.dtype)


def rope_partial(x, pos):
    half = ROT_DIM // 2
    inv = ROPE_THETA ** (-jnp.arange(0, ROT_DIM, 2, dtype=jnp.float32) / ROT_DIM)
    ang = pos.astype(jnp.float32)[:, None] * inv[None, :]
    cos = jnp.cos(ang)[None, :, None, :]
    sin = jnp.sin(ang)[None, :, None, :]
    x1 = x[..., :half].astype(jnp.float32)
    x2 = x[..., half:ROT_DIM].astype(jnp.float32)
    r = jnp.concatenate([x1 * cos - x2 * sin, x2 * cos + x1 * sin], axis=-1).astype(x.dtype)
    return jnp.concatenate([r, x[..., ROT_DIM:]], axis=-1)


def qkv_proj(h, w_qkv, b_qkv, pos):
    B, T, _ = h.shape
    qkv = h @ w_qkv + b_qkv
    q = qkv[..., :Q_DIM].reshape(B, T, N_HEADS, HEAD_DIM)
    k = qkv[..., Q_DIM:Q_DIM + KV_DIM].reshape(B, T, N_KV_HEADS, HEAD_DIM)
    v = qkv[..., Q_DIM + KV_DIM:].reshape(B, T, N_KV_HEADS, HEAD_DIM)
    return rope_partial(q, pos), rope_partial(k, pos), v


def sink_softmax(s, sink):
    sk = sink.astype(jnp.float32).reshape(N_KV_HEADS, GQA_GROUP)[:, :, None, None]
    m = jnp.maximum(jnp.max(s, axis=-1, keepdims=True), sk)
    e = jnp.exp(s - m)
    return e / (jnp.sum(e, axis=-1, keepdims=True) + jnp.exp(sk - m))


def attn_prompt(h, w_qkv, b_qkv, w_o, sinks):
    B, S, _ = h.shape
    nc = S // CHUNK
    q, k, v = qkv_proj(h, w_qkv, b_qkv, jnp.arange(S, dtype=jnp.int32))
    pad = ((0, 0), (WINDOW, 0), (0, 0), (0, 0))
    kp = jnp.pad(k, pad).reshape(B, nc + N_WIN_CHUNKS, CHUNK, N_KV_HEADS, HEAD_DIM)
    vp = jnp.pad(v, pad).reshape(B, nc + N_WIN_CHUNKS, CHUNK, N_KV_HEADS, HEAD_DIM)
    kwin = jnp.concatenate([kp[:, j:j + nc] for j in range(N_WIN_CHUNKS + 1)], axis=2)
    vwin = jnp.concatenate([vp[:, j:j + nc] for j in range(N_WIN_CHUNKS + 1)], axis=2)
    L = WINDOW + CHUNK
    key_pos = jnp.arange(nc)[:, None] * CHUNK - WINDOW + jnp.arange(L)[None, :]
    valid = key_pos >= 0
    qb = q.reshape(B, nc, CHUNK, N_KV_HEADS, GQA_GROUP, HEAD_DIM)
    s = jnp.einsum('bnqkgd,bnlkd->bnkgql', qb, kwin).astype(jnp.float32) * (1.0 / math.sqrt(HEAD_DIM))
    s = jnp.where(valid[None, :, None, None, None, :], s, -jnp.inf)
    p = sink_softmax(s, sinks).astype(v.dtype)
    o = jnp.einsum('bnkgql,bnlkd->bnqkgd', p, vwin).reshape(B, S, Q_DIM)
    return o @ w_o, k[:, -WINDOW:], v[:, -WINDOW:]


def attn_sample(h, ck, cv, w_qkv, b_qkv, w_o, sinks):
    B, T, _ = h.shape
    pos = PAST_LEN + jnp.arange(T, dtype=jnp.int32)
    q, k, v = qkv_proj(h, w_qkv, b_qkv, pos)
    kall = jnp.concatenate([ck.astype(k.dtype), k], axis=1)
    vall = jnp.concatenate([cv.astype(v.dtype), v], axis=1)
    qb = q.reshape(B, T, N_KV_HEADS, GQA_GROUP, HEAD_DIM)
    s = jnp.einsum('btkgd,blkd->bkgtl', qb, kall).astype(jnp.float32) * (1.0 / math.sqrt(HEAD_DIM))
    p = sink_softmax(s, sinks).astype(v.dtype)
    o = jnp.einsum('bkgtl,blkd->btkgd', p, vall).reshape(B, T, Q_DIM)
    return o @ w_o, k, v


def pool_mix(h, hist, pos0, w_pool, scale):
    B, T, D = h.shape
    xx = jnp.concatenate([hist.astype(h.dtype), h], axis=1)
    csp = jnp.pad(jnp.cumsum(xx.astype(jnp.float32), axis=1), ((0, 0), (1, 0), (0, 0)))
    pos = pos0 + jnp.arange(T)
    base = POOL_HIST + 1
    means = []
    for g, w in enumerate(POOL_WINDOWS):
        sl = slice(g * POOL_GROUP_DIM, (g + 1) * POOL_GROUP_DIM)
        win_sum = csp[:, base:base + T, sl] - csp[:, base - w:base - w + T, sl]
        cnt = jnp.minimum(pos + 1, w).astype(jnp.float32)[None, :, None]
        means.append(win_sum / cnt)
    d = (jnp.concatenate(means, axis=-1) - h.astype(jnp.float32)).astype(h.dtype)
    d = d.reshape(B, T, N_POOL_GROUPS, POOL_GROUP_DIM)
    y = jnp.einsum('btgc,gcd->btgd', d, w_pool).reshape(B, T, D) * scale
    return y, xx[:, -POOL_HIST:]


def ffn_and_ple(x, p, g_pre, g_post, w_up, w_down, w_ple_proj, w_ple_gate):
    h = rms_norm(x, g_pre)
    u = jax.nn.relu(h @ w_up)
    x = x + rms_norm((u * u) @ w_down, g_post)
    gate = jax.nn.sigmoid((x @ w_ple_gate).astype(jnp.float32)).astype(x.dtype)
    return x + gate * (p @ w_ple_proj)


def setup_inputs(seed: int = 0) -> dict:
    key = jax.random.key(seed)
    ks = jax.random.split(key, 24)
    f32 = jnp.float32
    nrm = lambda k, shape, s: jax.random.normal(k, shape, f32) * s
    return {
        "x_prompt": nrm(ks[0], (BATCH, SEQ, D_MODEL), 1.0),
        "x_sample": nrm(ks[1], (DEC_BATCH, DEC_SEQ, D_MODEL), 1.0),
        "cache_k": nrm(ks[2], (N_ATTN_LAYERS, DEC_BATCH, WINDOW, N_KV_HEADS, HEAD_DIM), 1.0),
        "cache_v": nrm(ks[3], (N_ATTN_LAYERS, DEC_BATCH, WINDOW, N_KV_HEADS, HEAD_DIM), 1.0),
        "state_pool": nrm(ks[4], (N_POOL_LAYERS, DEC_BATCH, POOL_HIST, D_MODEL), 1.0),
        "p_prompt": nrm(ks[5], (DEPTH, BATCH, SEQ, PLE_DIM), 1.0),
        "p_sample": nrm(ks[6], (DEPTH, DEC_BATCH, DEC_SEQ, PLE_DIM), 1.0),
        "norm_mix_pre": 1.0 + nrm(ks[7], (DEPTH, D_MODEL), 0.05),
        "norm_mix_post": 1.0 + nrm(ks[8], (DEPTH, D_MODEL), 0.05),
        "norm_ffn_pre": 1.0 + nrm(ks[9], (DEPTH, D_MODEL), 0.05),
        "norm_ffn_post": 1.0 + nrm(ks[10], (DEPTH, D_MODEL), 0.05),
        "w_qkv": nrm(ks[11], (N_ATTN_LAYERS, D_MODEL, QKV_DIM), D_MODEL ** -0.5),
        "b_qkv": nrm(ks[12], (N_ATTN_LAYERS, QKV_DIM), 0.02),
        "w_o": nrm(ks[13], (N_ATTN_LAYERS, Q_DIM, D_MODEL), Q_DIM ** -0.5),
        "sinks": nrm(ks[14], (N_ATTN_LAYERS, N_HEADS), 0.5),
        "w_pool": nrm(ks[15], (N_POOL_LAYERS, N_POOL_GROUPS, POOL_GROUP_DIM, POOL_GROUP_DIM), POOL_GROUP_DIM ** -0.5),
        "pool_scale": 1.0 + nrm(ks[16], (N_POOL_LAYERS, D_MODEL), 0.1),
        "w_ffn_up": nrm(ks[17], (DEPTH, D_MODEL, D_FF), D_MODEL ** -0.5),
        "w_ffn_down": nrm(ks[18], (DEPTH, D_FF, D_MODEL), D_FF ** -0.5),
        "w_ple_proj": nrm(ks[19], (DEPTH, PLE_DIM, D_MODEL), PLE_DIM ** -0.5),
        "w_ple_gate": nrm(ks[20], (DEPTH, D_MODEL, D_MODEL), D_MODEL ** -0.5),
    }


def reference(x_prompt, x_sample, cache_k, cache_v, state_pool, p_prompt, p_sample,
              norm_mix_pre, norm_mix_post, norm_ffn_pre, norm_ffn_post,
              w_qkv, b_qkv, w_o, sinks, w_pool, pool_scale,
              w_ffn_up, w_ffn_down, w_ple_proj, w_ple_gate):
    xp, xs = x_prompt, x_sample
    kp_list, vp_list, sp_list = [], [], []
    ks_list, vs_list, ss_list = [], [], []
    for i in range(DEPTH):
        hp = rms_norm(xp, norm_mix_pre[i])
        hs = rms_norm(xs, norm_mix_pre[i])
        j = i // N_MIXERS
        if i % N_MIXERS == 0:
            mp, k_p, v_p = attn_prompt(hp, w_qkv[j], b_qkv[j], w_o[j], sinks[j])
            ms, k_s, v_s = attn_sample(hs, cache_k[j], cache_v[j], w_qkv[j], b_qkv[j], w_o[j], sinks[j])
            kp_list.append(k_p); vp_list.append(v_p)
            ks_list.append(k_s); vs_list.append(v_s)
        else:
            zero_hist = jnp.zeros((hp.shape[0], POOL_HIST, D_MODEL), hp.dtype)
            mp, s_p = pool_mix(hp, zero_hist, 0, w_pool[j], pool_scale[j])
            ms, s_s = pool_mix(hs, state_pool[j], PAST_LEN, w_pool[j], pool_scale[j])
            sp_list.append(s_p); ss_list.append(s_s)
        xp = xp + rms_norm(mp, norm_mix_post[i])
        xs = xs + rms_norm(ms, norm_mix_post[i])
        xp = ffn_and_ple(xp, p_prompt[i], norm_ffn_pre[i], norm_ffn_post[i], w_ffn_up[i], w_ffn_down[i], w_ple_proj[i], w_ple_gate[i])
        xs = ffn_and_ple(xs, p_sample[i], norm_ffn_pre[i], norm_ffn_post[i], w_ffn_up[i], w_ffn_down[i], w_ple_proj[i], w_ple_gate[i])
    new_k_prompt = jnp.stack(kp_list)
    new_v_prompt = jnp.stack(vp_list)
    new_pool_prompt = jnp.stack(sp_list)
    new_k_sample = jnp.stack(ks_list)
    new_v_sample = jnp.stack(vs_list)
    new_pool_sample = jnp.stack(ss_list)
    return (xp, xs, new_k_prompt, new_v_prompt, new_pool_prompt, new_k_sample, new_v_sample, new_pool_sample)
```

```python
import numpy as np
from contextlib import ExitStack
import concourse.bass as bass
import concourse.mybir as mybir
from concourse.bass_utils import run_bass_kernel_spmd

F32 = mybir.dt.float32
BF16 = mybir.dt.bfloat16
AF = mybir.ActivationFunctionType
ALU = mybir.AluOpType

D = 2048
NK = 16
T = 704
TW = (384, 320)
TO = (0, 384)
NPASS = 3
NCH = 11
NSLOT = 15
G = 8
NGRP = 64 // G
EPS = 1e-6
POOLW = (2, 4, 8, 16)


def _k16(W, cols):
    return np.ascontiguousarray(W[:, cols].reshape(16, 128, len(cols)).transpose(1, 0, 2)).reshape(128, -1)


def build_wstream(w_qkv, w_o, w_pool, w_up, w_down, w_ple, w_gate):
    slabs = []
    ar = np.arange
    Wq = w_qkv[0]
    slabs.append(_k16(Wq, 2048 + ar(256)))
    swp = np.concatenate([64 + ar(64), ar(64), 192 + ar(64), 128 + ar(64)])
    slabs.append(_k16(Wq, 2048 + swp))
    slabs.append(_k16(Wq, 2304 + ar(256)))
    for g in range(4):
        for s in range(2):
            slabs.append(_k16(Wq, 512 * g + 256 * s + ar(256)))
        for s in range(2):
            blk = w_o[0][512 * g:512 * g + 512, 1024 * s:1024 * s + 1024]
            slabs.append(np.ascontiguousarray(blk.reshape(4, 128, 1024).transpose(1, 0, 2)).reshape(128, -1))

    def ffn_tail(l):
        for grp in range(NGRP):
            for s in range(G // 2):
                slabs.append(_k16(w_up[l], grp * G * 128 + 256 * s + ar(256)))
            ncol = 4096 // G
            for s in range(2048 // ncol):
                blk = w_down[l][grp * G * 128:(grp + 1) * G * 128, ncol * s:ncol * (s + 1)]
                slabs.append(np.ascontiguousarray(blk.reshape(G, 128, ncol).transpose(1, 0, 2)).reshape(128, -1))
        slabs.append(np.ascontiguousarray(w_ple[l].reshape(2, 128, 2048).transpose(1, 0, 2)).reshape(128, -1))
        for s in range(8):
            slabs.append(_k16(w_gate[l], 256 * s + ar(256)))

    ffn_tail(0)
    for s in range(2):
        blk = w_pool[0][2 * s:2 * s + 2]
        slabs.append(np.ascontiguousarray(blk.reshape(2, 4, 128, 512).transpose(2, 0, 1, 3)).reshape(128, -1))
    ffn_tail(1)
    return np.ascontiguousarray(np.stack(slabs).astype(np.float32))


NSLAB_L0 = 3 + 4 * 4 + NGRP * (G // 2 + 2048 // (4096 // G)) + 1 + 8
NSLAB_L1 = 2 + NGRP * (G // 2 + 2048 // (4096 // G)) + 1 + 8
NSLAB = NSLAB_L0 + NSLAB_L1


class Eng:
    def __init__(self, h, sem):
        self.h, self.sem, self.n, self.seen, self.is_pe = h, sem, 0, {}, False


class Tl:
    def __init__(self, ap, dsem=None):
        self.ap, self.w, self.r, self.dsem, self.dn, self.psum = ap, None, {}, dsem, 0, False


def _waits(E, rd, wr):
    need = []
    for t in rd:
        if t.w is not None:
            need.append((t.w, True))
    for t in wr:
        if t.w is not None:
            need.append((t.w, False))
        for ev in t.r.values():
            need.append((ev, False))
    for (sem, val, eng), raw in need:
        if eng is E and E.is_pe:
            continue
        key = id(sem)
        if E.seen.get(key, 0) >= val:
            continue
        E.h.wait_ge(sem, val)
        E.seen[key] = val


def _commit(ev, rd, wr):
    for t in rd:
        t.r[id(ev[0])] = ev
    for t in wr:
        t.w = ev
        t.r = {}


def op(E, fn, rd=(), wr=()):
    if any(t.psum for t in rd):
        wr = list(wr) + [t for t in rd if t.psum and t not in wr]
        rd = [t for t in rd if not t.psum]
    _waits(E, rd, wr)
    ins = fn()
    E.n += 1
    ins.then_inc(E.sem, 1)
    _commit((E.sem, E.n, E), rd, wr)


def dma(Q, out_ap, in_ap, rd=(), wr=(), ev_tile=None):
    _waits(Q, rd, wr)
    ins = Q.h.dma_start(out=out_ap, in_=in_ap)
    t = ev_tile if ev_tile is not None else (list(wr) + list(rd))[0]
    t.dn += 1
    ins.then_inc(t.dsem, 16)
    ev = (t.dsem, 16 * t.dn, None)
    _commit(ev, rd, wr)
    return ev


class StopBuild(Exception):
    pass


def build_program(stop=None):
    nc = bass.Bass("TRN2", target_bir_lowering=False)
    din = lambda n, s: nc.dram_tensor(n, s, F32, kind="ExternalInput").ap()
    dout = lambda n, s: nc.dram_tensor(n, s, F32, kind="ExternalOutput").ap()
    xT_d = din("xT", [NPASS, NK, 128, T])
    pT_d = din("pT", [2, NPASS, 2, 128, T])
    wst_d = din("wst", [NSLAB, 128, 4096])
    tabC_d = din("tabC", [NPASS, 128, T])
    tabS_d = din("tabS", [NPASS, 128, T])
    ckA_d = din("ckA", [2, 128, 128])
    ckB_d = din("ckB", [2, 128, 128])
    cv_d = din("cv", [2, 64, 256])
    spT_d = din("spT", [128, NK, 16])
    gains_d = din("gains", [128, 8 * NK])
    pscale_d = din("pscale", [128, NK])
    bq_d = din("bq", [128, 20])
    bv_d = din("bv", [1, 256])
    sinks_d = din("sinks", [1, 32])
    rt_d = din("rt", [128, 128])
    icnt_d = din("icnt", [128, 4, 16])
    yT_d = dout("yT", [NPASS, NK, 128, T])
    kout_d = dout("kout", [2, 128, 192])
    vout_d = dout("vout", [3, 64, 256])
    pout_d = dout("pout", [128, NK, 32])

    with ExitStack() as es:
        sb = lambda n, s, dt: es.enter_context(nc.sbuf_tensor(n, s, dt))
        sem = lambda n: es.enter_context(nc.semaphore(n))
        PE = Eng(nc.tensor, sem("s_pe"))
        PE.is_pe = True
        ACT = Eng(nc.scalar, sem("s_act"))
        DVE = Eng(nc.vector, sem("s_dve"))
        POOL = Eng(nc.gpsimd, sem("s_pool"))
        SP = Eng(nc.sync, sem("s_sp"))
        pe, act, dve, pool = nc.tensor, nc.scalar, nc.vector, nc.gpsimd

        def tiles2(tensor, n):
            return [[Tl(tensor[:, k, TO[t]:TO[t] + TW[t]]) for t in range(2)] for k in range(n)]

        X_t = sb("X", [128, NK, T], F32)
        Y_t = sb("Y", [128, NK, T], F32)
        H_t = sb("H", [128, NK, T], BF16)
        U_t = sb("U", [128, G, T], BF16)
        X, Y, H, U = tiles2(X_t, NK), tiles2(Y_t, NK), tiles2(H_t, NK), tiles2(U_t, G)
        xsem = [sem(f"s_x{k}") for k in range(NK)]
        ysem = [sem(f"s_y{k}") for k in range(NK)]
        for k in range(NK):
            X[k][0].dsem = xsem[k]
            Y[k][0].dsem = ysem[k]
        NSL = 3
        slab_t = [sb(f"slab{i}", [128, 4096], BF16) for i in range(NSL)]
        slabs = [Tl(slab_t[i][:], sem(f"s_slab{i}")) for i in range(NSL)]
        KA_t = [sb(f"KA{j}", [128, NSLOT * 64], BF16) for j in range(2)]
        KB_t = [sb(f"KB{j}", [128, NSLOT * 64], BF16) for j in range(2)]
        KA = [Tl(KA_t[j][:], sem(f"s_ka{j}")) for j in range(2)]
        KB = [Tl(KB_t[j][:], sem(f"s_kb{j}")) for j in range(2)]
        V_t = sb("V", [64, NSLOT, 256], BF16)
        Vt = Tl(V_t[:], sem("s_v"))
        PT_t = [sb(f"PT{j}", [64, 3, 512], BF16) for j in range(2)]
        PT = [Tl(PT_t[j][:]) for j in range(2)]
        RD_t = sb("RD", [128, 512], F32)
        RD = Tl(RD_t[:])
        TC_t = sb("TC", [128, T], F32)
        TS_t = sb("TS", [128, T], F32)
        TC, TS = Tl(TC_t[:], sem("s_tc")), Tl(TS_t[:], sem("s_ts"))
        PPT_t = sb("PPT", [128, 2, T], BF16)
        PPT = Tl(PPT_t[:], sem("s_ppt"))
        SQ_t = [sb(f"SQ{i}", [128, 384], BF16) for i in range(4)]
        SQ = [Tl(SQ_t[i][:]) for i in range(4)]
        RS_t = [sb(f"RS{i}", [128, 384], F32) for i in range(2)]
        RS = [Tl(RS_t[i][:]) for i in range(2)]
        TMP_t = [sb(f"TMP{i}", [128, 400], F32) for i in range(8)]
        TMP = [Tl(TMP_t[i][:], sem(f"s_tmp{i}")) for i in range(8)]
        PH_t = sb("PH", [128, NK, 16], F32)
        PH = Tl(PH_t[:])
        SPH_t = sb("SPH", [128, NK, 16], F32)
        SPH = Tl(SPH_t[:], sem("s_sph"))
        PO_t = sb("PO", [128, NK, 32], F32)
        PO = Tl(PO_t[:], sem("s_po"))
        VO_t = sb("VO", [64, 3, 256], F32)
        VO = Tl(VO_t[:], sem("s_vo"))
        CONST_t = sb("CONST", [128, 8 * NK + NK + 20], F32)
        CONST = Tl(CONST_t[:], sem("s_const"))
        GN = lambda i, k: CONST_t[:, i * NK + k:i * NK + k + 1]
        PSC = lambda k: CONST_t[:, 8 * NK + k:8 * NK + k + 1]
        BQ = lambda c: CONST_t[:, 9 * NK + c:9 * NK + c + 1]
        RT_t = sb("RT", [128, 128], F32)
        RT = Tl(RT_t[:], sem("s_rt"))
        IC_t = sb("IC", [128, 4, 16], F32)
        IC = Tl(IC_t[:], sem("s_ic"))
        ROW_t = sb("ROW", [1, 256 + 32], F32)
        ROW = Tl(ROW_t[:], sem("s_row"))
        ONES_t = sb("ONES", [128, 128], BF16)
        ONES32_t = sb("ONES32", [1, 128], F32)
        ONES = Tl(ONES_t[:])
        PS_t = [es.enter_context(nc.psum_tensor(f"ps{i}", [128, 512], F32)) for i in range(8)]
        PS = [Tl(PS_t[i][:]) for i in range(8)]
        for t_ in PS:
            t_.psum = True

        def bc64(a):
            return bass.AP(a.tensor, a.offset, [list(x) for x in a.ap] + [[0, 64]])

        def chk(name):
            if stop == name:
                raise StopBuild()

        cnt = {"ps": 0, "sq": 0, "rs": 0, "tmp": 0, "slab": 0, "issued": 0}

        def nxt(pool_, key):
            i = cnt[key]
            cnt[key] += 1
            return pool_[i % len(pool_)]

        op(DVE, lambda: dve.memset(ONES_t[:], 1.0), wr=[ONES])
        op(DVE, lambda: dve.memset(ONES32_t[:], 1.0), wr=[ONES])
        op(DVE, lambda: dve.memset(PH_t[:], 0.0), wr=[PH])
        op(DVE, lambda: dve.memset(PO_t[:], 0.0), wr=[PO])
        dma(SP, CONST_t[:, 0:8 * NK], gains_d, wr=[CONST])
        dma(SP, CONST_t[:, 8 * NK:9 * NK], pscale_d, wr=[CONST])
        dma(SP, CONST_t[:, 9 * NK:9 * NK + 20], bq_d, wr=[CONST])
        dma(SP, RT_t[:], rt_d, wr=[RT])
        dma(SP, IC_t[:], icnt_d, wr=[IC])
        dma(SP, ROW_t[:, 0:256], bv_d, wr=[ROW])
        dma(SP, ROW_t[:, 256:], sinks_d, wr=[ROW])
        dma(SP, SPH_t[:], spT_d, wr=[SPH])
        op(ACT, lambda: act.activation(out=ROW_t[:, 256:], in_=ROW_t[:, 256:], func=AF.Exp), rd=[ROW], wr=[ROW])
        for j in range(2):
            dma(POOL, KA_t[j][:, 13 * 64:15 * 64], ckA_d[j], wr=[KA[j]])
            dma(POOL, KB_t[j][:, 13 * 64:15 * 64], ckB_d[j], wr=[KB[j]])
            dma(POOL, V_t[:, 13 + j, :], cv_d[j], wr=[Vt])

        chk("const")

        def issue_upto(n):
            while cnt["issued"] <= n and cnt["issued"] < NPASS * NSLAB:
                i = cnt["issued"]
                s = slabs[i % NSL]
                dma(POOL, s.ap.rearrange("p (a b) -> p a b", a=2), wst_d[i % NSLAB].rearrange("p (a b) -> p a b", a=2), wr=[s])
                cnt["issued"] += 1

        def next_slab():
            i = cnt["slab"]
            cnt["slab"] += 1
            issue_upto(i + NSL - 1)
            return slabs[i % NSL], slab_t[i % NSL]

        def mmgroup(out_ap, pairs, rd, ps, first_extra=None):
            def fn():
                ins = None
                n = len(pairs)
                if first_extra is not None:
                    ins = pe.matmul(out_ap, first_extra[0], first_extra[1], start=True, stop=False)
                for i, (l, r) in enumerate(pairs):
                    ins = pe.matmul(out_ap, l, r, start=(i == 0 and first_extra is None), stop=(i == n - 1))
                return ins
            op(PE, fn, rd=rd, wr=[ps])

        def stats(src, t, act_only=False):
            w = TW[t]
            ps = nxt(PS, "ps")
            for k in range(NK):
                sq = nxt(SQ, "sq")
                if k % 2 == 0 or act_only:
                    op(ACT, lambda: act.activation(out=sq.ap[:, 0:w], in_=src[k][t].ap, func=AF.Square),
                       rd=[src[k][t]], wr=[sq])
                else:
                    op(DVE, lambda: dve.tensor_tensor(out=sq.ap[:, 0:w], in0=src[k][t].ap, in1=src[k][t].ap, op=ALU.mult),
                       rd=[src[k][t]], wr=[sq])
                op(PE, lambda: pe.matmul(ps.ap[:, 0:w], ONES_t[:], sq.ap[:, 0:w], start=(k == 0), stop=(k == NK - 1)),
                   rd=[sq, ONES], wr=[ps])
            rs = nxt(RS, "rs")
            op(ACT, lambda: act.activation(out=rs.ap[:, 0:w], in_=ps.ap[:, 0:w], func=AF.Ln, bias=EPS, scale=1.0 / D),
               rd=[ps], wr=[rs])
            op(ACT, lambda: act.activation(out=rs.ap[:, 0:w], in_=rs.ap[:, 0:w], func=AF.Exp, scale=-0.5), rd=[rs], wr=[rs])
            return rs

        def norm_to_h(gi):
            for t in range(2):
                rs = stats(X, t, act_only=True)
                w = TW[t]
                for k in range(NK):
                    op(DVE, lambda: dve.scalar_tensor_tensor(out=H[k][t].ap, in0=X[k][t].ap, scalar=GN(gi, k),
                                                             in1=rs.ap[:, 0:w], op0=ALU.mult, op1=ALU.mult),
                       rd=[X[k][t], rs, CONST], wr=[H[k][t]])

        def post_norm_add(gi):
            rss = [stats(Y, t, act_only=True) for t in range(2)]
            for t in range(2):
                rs = rss[t]
                w = TW[t]
                for k in range(NK):
                    tmp = nxt(TMP, "tmp")
                    op(DVE, lambda: dve.scalar_tensor_tensor(out=tmp.ap[:, 0:w], in0=Y[k][t].ap, scalar=GN(gi, k),
                                                             in1=rs.ap[:, 0:w], op0=ALU.mult, op1=ALU.mult),
                       rd=[Y[k][t], rs, CONST], wr=[tmp])
                    op(POOL, lambda: pool.tensor_tensor(out=X[k][t].ap, in0=X[k][t].ap, in1=tmp.ap[:, 0:w], op=ALU.add),
                       rd=[X[k][t], tmp], wr=[X[k][t]])

        def acc_y(m, t, ps, first):
            w = TW[t]
            if first:
                op(ACT, lambda: act.copy(out=Y[m][t].ap, in_=ps.ap[:, 0:w]), rd=[ps], wr=[Y[m][t]])
            else:
                op(DVE, lambda: dve.tensor_tensor(out=Y[m][t].ap, in0=Y[m][t].ap, in1=ps.ap[:, 0:w], op=ALU.add),
                   rd=[ps, Y[m][t]], wr=[Y[m][t]])

        def ffn_and_ple(l, ps_i):
            dma(POOL, PPT_t[:], pT_d[l, ps_i].rearrange("k p t -> p k t"), wr=[PPT])
            norm_to_h(4 * l + 2)
            for grp in range(NGRP):
                for s in range(G // 2):
                    sl, slt = next_slab()
                    v = slt[:].rearrange("p (k c) -> p k c", k=16)
                    for jj in range(2):
                        j = 2 * s + jj
                        for t in range(2):
                            w = TW[t]
                            ps = nxt(PS, "ps")
                            mmgroup(ps.ap[:, 0:w], [(v[:, k, jj * 128:(jj + 1) * 128], H[k][t].ap) for k in range(NK)],
                                    [sl] + [H[k][t] for k in range(NK)], ps)
                            tmp = nxt(TMP, "tmp")
                            op(ACT, lambda: act.activation(out=tmp.ap[:, 0:w], in_=ps.ap[:, 0:w], func=AF.Relu),
                               rd=[ps], wr=[tmp])
                            op(DVE, lambda: dve.tensor_tensor(out=U[j][t].ap, in0=tmp.ap[:, 0:w], in1=tmp.ap[:, 0:w], op=ALU.mult),
                               rd=[tmp], wr=[U[j][t]])
                ncol = 4096 // G
                for s in range(2048 // ncol):
                    sl, slt = next_slab()
                    v = slt[:].rearrange("p (j c) -> p j c", j=G)
                    for mm in range(ncol // 128):
                        m = s * (ncol // 128) + mm
                        for t in range(2):
                            w = TW[t]
                            ps = nxt(PS, "ps")
                            mmgroup(ps.ap[:, 0:w], [(v[:, j, mm * 128:(mm + 1) * 128], U[j][t].ap) for j in range(G)],
                                    [sl] + [U[j][t] for j in range(G)], ps)
                            acc_y(m, t, ps, grp == 0)
            chk(f"ffnraw{l}")
            post_norm_add(4 * l + 3)
            chk(f"ffn{l}")
            for k in range(NK):
                for t in range(2):
                    op(ACT, lambda: act.copy(out=H[k][t].ap, in_=X[k][t].ap), rd=[X[k][t]], wr=[H[k][t]])
            psl, pslt = next_slab()
            pv = pslt[:].rearrange("p (k c) -> p k c", k=2)
            for m in range(NK):
                for t in range(2):
                    w = TW[t]
                    ps2 = nxt(PS, "ps")
                    mmgroup(ps2.ap[:, 0:w], [(pv[:, kk, m * 128:(m + 1) * 128], PPT_t[:, kk, TO[t]:TO[t] + w]) for kk in range(2)],
                            [psl, PPT], ps2)
                    op(ACT, lambda: act.copy(out=Y[m][t].ap, in_=ps2.ap[:, 0:w]), rd=[ps2], wr=[Y[m][t]])
            for s in range(8):
                sl, slt = next_slab()
                v = slt[:].rearrange("p (k c) -> p k c", k=16)
                for mm in range(2):
                    m = 2 * s + mm
                    for t in range(2):
                        w = TW[t]
                        ps = nxt(PS, "ps")
                        mmgroup(ps.ap[:, 0:w], [(v[:, k, mm * 128:(mm + 1) * 128], H[k][t].ap) for k in range(NK)],
                                [sl] + [H[k][t] for k in range(NK)], ps)
                        gt = nxt(TMP, "tmp")
                        op(ACT, lambda: act.activation(out=gt.ap[:, 0:w], in_=ps.ap[:, 0:w], func=AF.Sigmoid), rd=[ps], wr=[gt])
                        op(DVE, lambda: dve.tensor_tensor(out=gt.ap[:, 0:w], in0=gt.ap[:, 0:w], in1=Y[m][t].ap, op=ALU.mult),
                           rd=[gt, Y[m][t]], wr=[gt])
                        op(DVE, lambda: dve.tensor_tensor(out=X[m][t].ap, in0=X[m][t].ap, in1=gt.ap[:, 0:w], op=ALU.add),
                           rd=[X[m][t], gt], wr=[X[m][t]])

        pend = []

        def flush_rope():
            while pend:
                pend.pop(0)()

        def rope_chunk(ps, bias_col, t, dst_ap, dst_tile, keep32=None):
            w, c0 = TW[t], TO[t]
            qb = nxt(TMP, "tmp")
            op(ACT, lambda: act.activation(out=qb.ap[:, 0:w], in_=ps.ap[:, 0:w], func=AF.Identity, bias=BQ(bias_col), scale=1.0),
               rd=[ps, CONST], wr=[qb])
            prev = list(pend)
            del pend[:]
            pend.append(lambda: rope_tail(qb, t, dst_ap, dst_tile, keep32))
            for f_ in prev:
                f_()

        def rope_tail(qb, t, dst_ap, dst_tile, keep32):
            w, c0 = TW[t], TO[t]
            ps2 = nxt(PS, "ps")
            op(PE, lambda: pe.matmul(ps2.ap[:, 0:w], RT_t[:], qb.ap[:, 0:w], start=True, stop=True), rd=[RT, qb], wr=[ps2])
            t2 = nxt(TMP, "tmp")
            op(DVE, lambda: dve.tensor_tensor(out=t2.ap[:, 0:w], in0=ps2.ap[:, 0:w], in1=TS_t[:, c0:c0 + w], op=ALU.mult),
               rd=[ps2, TS], wr=[t2])
            op(DVE, lambda: dve.tensor_tensor(out=qb.ap[:, 0:w], in0=qb.ap[:, 0:w], in1=TC_t[:, c0:c0 + w], op=ALU.mult),
               rd=[qb, TC], wr=[qb])
            if keep32 is not None:
                op(DVE, lambda: dve.tensor_tensor(out=keep32.ap[:, 0:w], in0=qb.ap[:, 0:w], in1=t2.ap[:, 0:w], op=ALU.add),
                   rd=[qb, t2], wr=[keep32])
                op(ACT, lambda: act.copy(out=dst_ap, in_=keep32.ap[:, 0:w]), rd=[keep32], wr=[dst_tile])
            else:
                op(DVE, lambda: dve.tensor_tensor(out=dst_ap, in0=qb.ap[:, 0:w], in1=t2.ap[:, 0:w], op=ALU.add),
                   rd=[qb, t2], wr=[dst_tile])

        def attention_layer(ps_i):
            chk("xload")
            norm_to_h(0)
            chk("norm")
            for kb in range(2):
                sl, slt = next_slab()
                v = slt[:].rearrange("p (k c) -> p k c", k=16)
                Kt_t, Kt = (KA_t, KA) if kb == 0 else (KB_t, KB)
                for j in range(2):
                    for t in range(2):
                        w = TW[t]
                        ps = nxt(PS, "ps")
                        mmgroup(ps.ap[:, 0:w], [(v[:, k, j * 128:(j + 1) * 128], H[k][t].ap) for k in range(NK)],
                                [sl] + [H[k][t] for k in range(NK)], ps)
                        keep = None
                        if kb == 0 and ps_i == 2 and t == 1:
                            keep = nxt(TMP, "tmp")
                        rope_chunk(ps, 16 + 2 * kb + j, t, Kt_t[j][:, 128 + TO[t]:128 + TO[t] + w], Kt[j], keep32=keep)
                        if keep is not None:
                            flush_rope()
                            dma(SP, kout_d[j], keep.ap[:, 128:320], rd=[keep])
            flush_rope()
            chk("k")
            sl, slt = next_slab()
            v = slt[:].rearrange("p (k c) -> p k c", k=16)
            for i in range(NCH):
                t = 0 if i < 6 else 1
                lc = 64 * i - TO[t]
                ps = nxt(PS, "ps")
                mmgroup(ps.ap[0:64, 0:256], [(H_t[:, k, 64 * i:64 * i + 64], v[:, k, :]) for k in range(NK)],
                        [sl, ROW] + [H[k][t] for k in range(NK)], ps,
                        first_extra=(ONES32_t[0:1, 0:64], ROW_t[0:1, 0:256]))
                op(ACT, lambda: act.copy(out=V_t[:, 2 + i, :], in_=ps.ap[0:64, 0:256]), rd=[ps], wr=[Vt])
                if ps_i == 2 and i >= 8:
                    op(ACT, lambda: act.copy(out=VO_t[:, i - 8, :], in_=ps.ap[0:64, 0:256]), rd=[ps], wr=[VO])
            if ps_i == 2:
                dma(SP, vout_d.rearrange("c p f -> p c f"), VO_t[:], rd=[VO])
            chk("v")
            for g in range(4):
                for s in range(2):
                    sl, slt = next_slab()
                    v = slt[:].rearrange("p (k c) -> p k c", k=16)
                    for jj in range(2):
                        c = 2 * s + jj
                        for t in range(2):
                            w = TW[t]
                            ps = nxt(PS, "ps")
                            mmgroup(ps.ap[:, 0:w], [(v[:, k, jj * 128:(jj + 1) * 128], H[k][t].ap) for k in range(NK)],
                                    [sl] + [H[k][t] for k in range(NK)], ps)
                            rope_chunk(ps, 4 * g + c, t, U[c][t].ap, U[c][t])
                flush_rope()
                chk("q0")
                Ke_t, Ke = (KA_t[g // 2], KA[g // 2]) if g % 2 == 0 else (KB_t[g // 2], KB[g // 2])
                Ko_t, Ko = (KB_t[g // 2], KB[g // 2]) if g % 2 == 0 else (KA_t[g // 2], KA[g // 2])
                def scores(i):
                    t = 0 if i < 6 else 1
                    qs = slice(64 * i, 64 * i + 64)
                    if ps_i == 2 and i == NCH - 1:
                        keys = [13, 14, 12]
                    else:
                        keys = [s_ for s_ in (i, i + 1, i + 2) if NCH * ps_i + s_ - 2 >= 0]
                    qrd = [U[c][t] for c in range(4)]
                    pt_t, ptl = PT_t[i % 2], PT[i % 2]
                    pts = [(jx, s_) for jx, s_ in enumerate(keys)]
                    for j0 in (0, 2):
                        grp_ = [p_ for p_ in pts if j0 <= p_[0] < j0 + 2]
                        if not grp_:
                            continue
                        n_ = len(grp_)
                        bE = nxt(PS, "ps")
                        bO = nxt(PS, "ps")
                        def fn():
                            ins = None
                            for jx, s_ in grp_:
                                ks = slice(64 * s_, 64 * s_ + 64)
                                cs_ = slice(256 * (jx - j0), 256 * (jx - j0) + 256)
                                pe.matmul(bE.ap[0:64, cs_], Ke_t[0:64, ks], U_t[0:64, 0:4, qs], start=True, stop=True)
                                ins = pe.matmul(bO.ap[0:64, cs_], Ko_t[64:128, ks], U_t[64:128, 0:4, qs], start=True, stop=True)
                            return ins
                        op(PE, fn, rd=[Ke, Ko] + qrd, wr=[bE, bO])
                        v3 = lambda ap_: ap_.rearrange("p (a b) -> p a b", a=n_)
                        op(ACT, lambda: act.activation(out=pt_t[:, j0:j0 + n_, 0:256], in_=v3(bE.ap[0:64, 0:256 * n_]), func=AF.Exp, scale=0.125),
                           rd=[bE], wr=[ptl])
                        op(ACT, lambda: act.activation(out=pt_t[:, j0:j0 + n_, 256:512], in_=v3(bO.ap[0:64, 0:256 * n_]), func=AF.Exp, scale=0.125),
                           rd=[bO], wr=[ptl])
                    return (t, qs, pt_t, ptl, pts)

                def finish(info):
                    t, qs, pt_t, ptl, pts = info
                    pdp = nxt(PS, "ps")
                    def fn_dp():
                        pe.matmul(pdp.ap[0:64, 0:256], ONES32_t[0:1, 0:64], bc64(ROW_t[0:1, 256 + 8 * g:256 + 8 * g + 4]), start=True, stop=False)
                        pe.matmul(pdp.ap[64:128, 0:256], ONES32_t[0:1, 0:64], bc64(ROW_t[0:1, 256 + 8 * g + 4:256 + 8 * g + 8]),
                                  start=True, stop=False, tile_position=(0, 64))
                        ins = None
                        for jx, s_ in pts:
                            last = jx == len(pts) - 1
                            pe.matmul(pdp.ap[0:64, 0:256], ONES_t[0:64, 0:64], pt_t[:, jx, 0:256], start=False, stop=last)
                            pe.matmul(pdp.ap[64:128, 0:256], ONES_t[0:64, 0:64], pt_t[:, jx, 256:512], start=False, stop=last,
                                      tile_position=(0, 64))
                        for jx, s_ in pts:
                            first, last = jx == 0, jx == len(pts) - 1
                            pe.matmul(pdp.ap[0:64, 256:512], V_t[:, s_, 64 * g:64 * g + 64], pt_t[:, jx, 0:256], start=first, stop=last)
                            ins = pe.matmul(pdp.ap[64:128, 256:512], V_t[:, s_, 64 * g:64 * g + 64], pt_t[:, jx, 256:512], start=first, stop=last,
                                            tile_position=(0, 64))
                        return ins
                    op(PE, fn_dp, rd=[ROW, ONES, ptl, Vt], wr=[pdp])
                    op(ACT, lambda: act.activation(out=RD_t[:, 0:256], in_=pdp.ap[:, 0:256], func=AF.Ln), rd=[pdp], wr=[RD])
                    op(ACT, lambda: act.activation(out=RD_t[:, 0:256], in_=RD_t[:, 0:256], func=AF.Exp, scale=-1.0), rd=[RD], wr=[RD])
                    r3 = lambda ap_: ap_.rearrange("p (a b) -> p a b", a=4)
                    op(DVE, lambda: dve.tensor_tensor(out=U_t[:, 4:8, qs], in0=r3(pdp.ap[:, 256:512]), in1=r3(RD_t[:, 0:256]), op=ALU.mult),
                       rd=[pdp, RD], wr=[U[4 + c][t] for c in range(4)])

                info = scores(0)
                for i in range(NCH):
                    nxt_info = scores(i + 1) if i + 1 < NCH else None
                    finish(info)
                    info = nxt_info
                for s in range(2):
                    sl, slt = next_slab()
                    v = slt[:].rearrange("p (k c) -> p k c", k=4)
                    for mm in range(8):
                        m = 8 * s + mm
                        for t in range(2):
                            w = TW[t]
                            ps = nxt(PS, "ps")
                            mmgroup(ps.ap[:, 0:w], [(v[:, c, mm * 128:(mm + 1) * 128], U[4 + c][t].ap) for c in range(4)],
                                    [sl] + [U[4 + c][t] for c in range(4)], ps)
                            acc_y(m, t, ps, g == 0)
            chk("attn")
            if ps_i < NPASS - 1:
                for j in range(2):
                    op(ACT, lambda: act.copy(out=KA_t[j][:, 0:128], in_=KA_t[j][:, 11 * 64:13 * 64]), rd=[KA[j]], wr=[KA[j]])
                    op(ACT, lambda: act.copy(out=KB_t[j][:, 0:128], in_=KB_t[j][:, 11 * 64:13 * 64]), rd=[KB[j]], wr=[KB[j]])
                op(ACT, lambda: act.copy(out=V_t[:, 0:2, :], in_=V_t[:, 11:13, :]), rd=[Vt], wr=[Vt])
            post_norm_add(1)
            chk("mix0")

        def pool_layer(ps_i):
            for t in range(2):
                rs = stats(X, t, act_only=True)
                w, c0 = TW[t], TO[t]
                for k in range(NK):
                    grp = k // 4
                    L = grp + 1
                    A = nxt(TMP, "tmp")
                    op(DVE, lambda: dve.tensor_copy(out=A.ap[:, 0:16], in_=PH_t[:, k, :]), rd=[PH], wr=[A])
                    op(DVE, lambda: dve.scalar_tensor_tensor(out=A.ap[:, 16:16 + w], in0=X[k][t].ap, scalar=GN(4, k),
                                                             in1=rs.ap[:, 0:w], op0=ALU.mult, op1=ALU.mult),
                       rd=[X[k][t], rs, CONST], wr=[A])
                    op(DVE, lambda: dve.tensor_copy(out=PH_t[:, k, :], in_=A.ap[:, w:w + 16]), rd=[A], wr=[PH])
                    if ps_i == 2 and t == 1:
                        op(DVE, lambda: dve.tensor_copy(out=PO_t[:, k, 0:16], in_=A.ap[:, 16 + 240:16 + 256]), rd=[A], wr=[PO])
                        op(DVE, lambda: dve.tensor_copy(out=PO_t[:, k, 16:32], in_=A.ap[:, 16 + 304:16 + 320]), rd=[A], wr=[PO])

                    def window(Abuf, n, out_ap, special):
                        src = Abuf
                        sh = 1
                        for step in range(L):
                            dst = nxt(TMP, "tmp")
                            lo = 2 * sh
                            op(DVE, lambda: dve.tensor_tensor(out=dst.ap[:, lo:16 + n], in0=src.ap[:, lo:16 + n],
                                                              in1=src.ap[:, lo - sh:16 + n - sh], op=ALU.add),
                               rd=[src], wr=[dst])
                            src = dst
                            sh *= 2
                        op(DVE, lambda: dve.scalar_tensor_tensor(out=out_ap, in0=src.ap[:, 16:16 + n], scalar=1.0 / POOLW[grp],
                                                                 in1=Abuf.ap[:, 16:16 + n], op0=ALU.mult, op1=ALU.subtract),
                           rd=[src, Abuf], wr=[H[k][t]])
                        if special:
                            tt = nxt(TMP, "tmp")
                            op(DVE, lambda: dve.tensor_tensor(out=tt.ap[:, 0:16], in0=src.ap[:, 16:32], in1=IC_t[:, grp, :], op=ALU.mult),
                               rd=[src, IC], wr=[tt])
                            op(DVE, lambda: dve.tensor_tensor(out=H_t[:, k, c0:c0 + 16], in0=tt.ap[:, 0:16], in1=Abuf.ap[:, 16:32], op=ALU.subtract),
                               rd=[tt, Abuf], wr=[H[k][t]])

                    window(A, w, H[k][t].ap, ps_i == 0 and t == 0)
                    if ps_i == 2 and t == 1:
                        As = nxt(TMP, "tmp")
                        op(DVE, lambda: dve.tensor_copy(out=As.ap[:, 0:16], in_=SPH_t[:, k, :]), rd=[SPH], wr=[As])
                        op(DVE, lambda: dve.tensor_copy(out=As.ap[:, 16:80], in_=A.ap[:, 16 + 256:16 + 320]), rd=[A], wr=[As])
                        window(As, 64, H_t[:, k, 640:704], False)
            for s in range(2):
                sl, slt = next_slab()
                v = slt[:].rearrange("p (g k c) -> p g k c", g=2, k=4)
                for gi in range(2):
                    g = 2 * s + gi
                    for mm in range(4):
                        m = 4 * g + mm
                        for t in range(2):
                            w = TW[t]
                            ps = nxt(PS, "ps")
                            mmgroup(ps.ap[:, 0:w], [(v[:, gi, k, mm * 128:(mm + 1) * 128], H[4 * g + k][t].ap) for k in range(4)],
                                    [sl] + [H[4 * g + k][t] for k in range(4)], ps)
                            op(ACT, lambda: act.activation(out=Y[m][t].ap, in_=ps.ap[:, 0:w], func=AF.Copy, scale=PSC(m)),
                               rd=[ps, CONST], wr=[Y[m][t]])
            if ps_i == 2:
                dma(SP, pout_d, PO_t[:], rd=[PO])
            post_norm_add(5)

        out_evs = []
        try:
          for ps_i in range(NPASS):
            for k in range(NK):
                dma(SP, X_t[:, k, :], xT_d[ps_i, k], wr=[X[k][0], X[k][1]], ev_tile=X[k][0])
            dma(SP, TC_t[:], tabC_d[ps_i], wr=[TC])
            dma(SP, TS_t[:], tabS_d[ps_i], wr=[TS])
            attention_layer(ps_i)
            chk(f"mix0_p{ps_i}")
            ffn_and_ple(0, ps_i)
            chk("l0")
            pool_layer(ps_i)
            chk("pool")
            ffn_and_ple(1, ps_i)
            chk(f"l1_p{ps_i}")
            for k in range(NK):
                ev = dma(SP, yT_d[ps_i, k], X_t[:, k, :], rd=[X[k][0], X[k][1]], ev_tile=X[k][0])
                out_evs.append(ev)
            X, Y = Y, X
            X_t, Y_t = Y_t, X_t
        except StopBuild:
            for E_ in (PE, ACT, DVE, POOL):
                if E_.n:
                    nc.sync.wait_ge(E_.sem, E_.n)
            for t_ in slabs:
                if t_.dn:
                    nc.sync.wait_ge(t_.dsem, 16 * t_.dn)
        for t_ in (VO, PO) + tuple(TMP):
            if t_.dn:
                out_evs.append((t_.dsem, 16 * t_.dn, None))
        for (s_, v_, _) in out_evs:
            nc.sync.wait_ge(s_, v_)
    return nc


_CACHE = {}


def host_inputs(x_prompt, x_sample, cache_k, cache_v, state_pool, p_prompt, p_sample,
                norm_mix_pre, norm_mix_post, norm_ffn_pre, norm_ffn_post,
                w_qkv, b_qkv, w_o, sinks, w_pool, pool_scale,
                w_ffn_up, w_ffn_down, w_ple_proj, w_ple_gate, cores=range(8)):
    f32 = lambda a: np.asarray(a, dtype=np.float32)
    x_prompt, x_sample, p_prompt, p_sample = f32(x_prompt), f32(x_sample), f32(p_prompt), f32(p_sample)
    cache_k, cache_v, state_pool = f32(cache_k), f32(cache_v), f32(state_pool)
    wst = build_wstream(f32(w_qkv), f32(w_o), f32(w_pool), f32(w_ffn_up), f32(w_ffn_down), f32(w_ple_proj), f32(w_ple_gate))
    pos = np.arange(NPASS * T, dtype=np.float32)
    inv = (np.float32(500000.0) ** (-np.arange(0, 16, 2, dtype=np.float32) / np.float32(16))).astype(np.float32)
    ang = pos[:, None] * inv[None, :]
    cs, sn = np.cos(ang).astype(np.float32), np.sin(ang).astype(np.float32)
    tabC = np.ones((NPASS * T, 128), np.float32)
    tabS = np.zeros((NPASS * T, 128), np.float32)
    for hb in (0, 64):
        tabC[:, hb:hb + 8] = cs
        tabC[:, hb + 8:hb + 16] = cs
        tabS[:, hb:hb + 8] = -sn
        tabS[:, hb + 8:hb + 16] = sn
    tabC = np.ascontiguousarray(tabC.reshape(NPASS, T, 128).transpose(0, 2, 1))
    tabS = np.ascontiguousarray(tabS.reshape(NPASS, T, 128).transpose(0, 2, 1))
    rt = np.zeros((128, 128), np.float32)
    for hb in (0, 64):
        for d in range(8):
            rt[hb + d + 8, hb + d] = 1.0
            rt[hb + d, hb + d + 8] = 1.0
    icnt = np.ones((128, 4, 16), np.float32)
    for gi, w in enumerate(POOLW):
        icnt[:, gi, :] = 1.0 / np.minimum(np.arange(16) + 1, w).astype(np.float32)
    vec = lambda v: np.ascontiguousarray(f32(v).reshape(NK, 128).T)
    gains = np.concatenate([vec(norm_mix_pre[0]), vec(norm_mix_post[0]), vec(norm_ffn_pre[0]), vec(norm_ffn_post[0]),
                            vec(norm_mix_pre[1]), vec(norm_mix_post[1]), vec(norm_ffn_pre[1]), vec(norm_ffn_post[1])], axis=1)
    pscale = vec(pool_scale[0])
    b = f32(b_qkv)[0]
    bk = b[2048:2304]
    bkswap = np.concatenate([bk[64:128], bk[0:64], bk[192:256], bk[128:192]])
    bq = np.ascontiguousarray(np.concatenate([b[0:2048], bk, bkswap]).reshape(20, 128).T)
    bv = np.ascontiguousarray(b[2304:2560].reshape(1, 256))
    sk = f32(sinks)[0]
    order = [8 * g + 2 * i + par for g in range(4) for par in range(2) for i in range(4)]
    sinks_exp = np.ascontiguousarray(sk[order].reshape(1, 32))

    in_maps = []
    for c in cores:
        seq = np.concatenate([x_prompt[c], x_sample[c]], axis=0)
        xT = np.ascontiguousarray(seq.reshape(NPASS, T, NK, 128).transpose(0, 2, 3, 1))
        pp = np.concatenate([p_prompt[:, c], p_sample[:, c]], axis=1)
        pT = np.ascontiguousarray(pp.reshape(2, NPASS, T, 2, 128).transpose(0, 1, 3, 4, 2))
        ck = cache_k[0, c].reshape(128, 256)
        ckA = np.ascontiguousarray(ck.T.reshape(2, 128, 128))
        ckB = np.ascontiguousarray(np.concatenate([ckA[:, 64:128], ckA[:, 0:64]], axis=1))
        cv = np.ascontiguousarray(cache_v[0, c].reshape(2, 64, 256))
        sp = state_pool[0, c]
        spT = np.zeros((128, NK, 16), np.float32)
        spT[:, :, 1:16] = sp.reshape(15, NK, 128).transpose(2, 1, 0)
        in_maps.append(dict(xT=xT, pT=pT, wst=wst, tabC=tabC, tabS=tabS, ckA=ckA, ckB=ckB, cv=cv, spT=spT,
                            gains=gains, pscale=pscale, bq=bq, bv=bv, sinks=sinks_exp, rt=rt, icnt=icnt))
    return in_maps


def assemble(R):
    y_prompt = np.zeros((8, 2048, D), np.float32)
    y_sample = np.zeros((8, 64, D), np.float32)
    nkp = np.zeros((1, 8, 128, 4, 64), np.float32)
    nvp = np.zeros((1, 8, 128, 4, 64), np.float32)
    npp = np.zeros((1, 8, 15, D), np.float32)
    nks = np.zeros((1, 8, 64, 4, 64), np.float32)
    nvs = np.zeros((1, 8, 64, 4, 64), np.float32)
    nps = np.zeros((1, 8, 15, D), np.float32)
    for c in range(8):
        yT = np.asarray(R[c]["yT"])
        seq = yT.transpose(0, 3, 1, 2).reshape(NPASS * T, D)
        y_prompt[c] = seq[:2048]
        y_sample[c] = seq[2048:]
        ko = np.asarray(R[c]["kout"]).reshape(256, 192).T
        nkp[0, c] = ko[:128].reshape(128, 4, 64)
        nks[0, c] = ko[128:].reshape(64, 4, 64)
        vo = np.asarray(R[c]["vout"]).reshape(192, 256)
        nvp[0, c] = vo[:128].reshape(128, 4, 64)
        nvs[0, c] = vo[128:].reshape(64, 4, 64)
        po = np.asarray(R[c]["pout"])
        npp[0, c] = po[:, :, 1:16].transpose(2, 1, 0).reshape(15, D)
        nps[0, c] = po[:, :, 17:32].transpose(2, 1, 0).reshape(15, D)
    return (y_prompt, y_sample, nkp, nvp, npp, nks, nvs, nps)


def kernel(**inputs):
    if "nc" not in _CACHE:
        _CACHE["nc"] = build_program()
    nc = _CACHE["nc"]
    in_maps = host_inputs(**inputs)
    res = run_bass_kernel_spmd(nc, in_maps, core_ids=list(range(8)))
    return assemble(res.results)
```

```python
import numpy as np
from contextlib import ExitStack
import concourse.bass as bass
import concourse.mybir as mybir
from concourse.bass_utils import run_bass_kernel_spmd

F32 = mybir.dt.float32
BF16 = mybir.dt.bfloat16
AF = mybir.ActivationFunctionType
ALU = mybir.AluOpType

D = 2048
NK = 16
T = 704
TW = (384, 320)
TO = (0, 384)
NPASS = 3
NCH = 11
NSLOT = 15
G = 8
NGRP = 64 // G
EPS = 1e-6
POOLW = (2, 4, 8, 16)


def _k16(W, cols):
    return np.ascontiguousarray(W[:, cols].reshape(16, 128, len(cols)).transpose(1, 0, 2)).reshape(128, -1)


def build_wstream(w_qkv, w_o, w_pool, w_up, w_down, w_ple, w_gate):
    slabs = []
    ar = np.arange
    Wq = w_qkv[0]
    slabs.append(_k16(Wq, 2048 + ar(256)))
    swp = np.concatenate([64 + ar(64), ar(64), 192 + ar(64), 128 + ar(64)])
    slabs.append(_k16(Wq, 2048 + swp))
    slabs.append(_k16(Wq, 2304 + ar(256)))
    for g in range(4):
        for s in range(2):
            slabs.append(_k16(Wq, 512 * g + 256 * s + ar(256)))
        for s in range(2):
            blk = w_o[0][512 * g:512 * g + 512, 1024 * s:1024 * s + 1024]
            slabs.append(np.ascontiguousarray(blk.reshape(4, 128, 1024).transpose(1, 0, 2)).reshape(128, -1))

    def ffn_tail(l):
        for grp in range(NGRP):
            for s in range(G // 2):
                slabs.append(_k16(w_up[l], grp * G * 128 + 256 * s + ar(256)))
            ncol = 4096 // G
            for s in range(2048 // ncol):
                blk = w_down[l][grp * G * 128:(grp + 1) * G * 128, ncol * s:ncol * (s + 1)]
                slabs.append(np.ascontiguousarray(blk.reshape(G, 128, ncol).transpose(1, 0, 2)).reshape(128, -1))
        slabs.append(np.ascontiguousarray(w_ple[l].reshape(2, 128, 2048).transpose(1, 0, 2)).reshape(128, -1))
        for s in range(8):
            slabs.append(_k16(w_gate[l], 256 * s + ar(256)))

    ffn_tail(0)
    for s in range(2):
        blk = w_pool[0][2 * s:2 * s + 2]
        slabs.append(np.ascontiguousarray(blk.reshape(2, 4, 128, 512).transpose(2, 0, 1, 3)).reshape(128, -1))
    ffn_tail(1)
    return np.ascontiguousarray(np.stack(slabs).astype(np.float32))


NSLAB_L0 = 3 + 4 * 4 + NGRP * (G // 2 + 2048 // (4096 // G)) + 1 + 8
NSLAB_L1 = 2 + NGRP * (G // 2 + 2048 // (4096 // G)) + 1 + 8
NSLAB = NSLAB_L0 + NSLAB_L1


class Eng:
    def __init__(self, h, sem):
        self.h, self.sem, self.n, self.seen, self.is_pe = h, sem, 0, {}, False


class Tl:
    def __init__(self, ap, dsem=None):
        self.ap, self.w, self.r, self.dsem, self.dn, self.psum = ap, None, {}, dsem, 0, False


def _waits(E, rd, wr):
    need = []
    for t in rd:
        if t.w is not None:
            need.append((t.w, True))
    for t in wr:
        if t.w is not None:
            need.append((t.w, False))
        for ev in t.r.values():
            need.append((ev, False))
    for (sem, val, eng), raw in need:
        if eng is E and E.is_pe:
            continue
        key = id(sem)
        if E.seen.get(key, 0) >= val:
            continue
        E.h.wait_ge(sem, val)
        E.seen[key] = val


def _commit(ev, rd, wr):
    for t in rd:
        t.r[id(ev[0])] = ev
    for t in wr:
        t.w = ev
        t.r = {}


def op(E, fn, rd=(), wr=()):
    if any(t.psum for t in rd):
        wr = list(wr) + [t for t in rd if t.psum and t not in wr]
        rd = [t for t in rd if not t.psum]
    _waits(E, rd, wr)
    ins = fn()
    E.n += 1
    ins.then_inc(E.sem, 1)
    _commit((E.sem, E.n, E), rd, wr)


def dma(Q, out_ap, in_ap, rd=(), wr=(), ev_tile=None):
    _waits(Q, rd, wr)
    ins = Q.h.dma_start(out=out_ap, in_=in_ap)
    t = ev_tile if ev_tile is not None else (list(wr) + list(rd))[0]
    t.dn += 1
    ins.then_inc(t.dsem, 16)
    ev = (t.dsem, 16 * t.dn, None)
    _commit(ev, rd, wr)
    return ev


class StopBuild(Exception):
    pass


def build_program(stop=None):
    nc = bass.Bass("TRN2", target_bir_lowering=False)
    din = lambda n, s: nc.dram_tensor(n, s, F32, kind="ExternalInput").ap()
    dout = lambda n, s: nc.dram_tensor(n, s, F32, kind="ExternalOutput").ap()
    xT_d = din("xT", [NPASS, NK, 128, T])
    pT_d = din("pT", [2, NPASS, 2, 128, T])
    wst_d = din("wst", [NSLAB, 128, 4096])
    tabC_d = din("tabC", [NPASS, 128, T])
    tabS_d = din("tabS", [NPASS, 128, T])
    ckA_d = din("ckA", [2, 128, 128])
    ckB_d = din("ckB", [2, 128, 128])
    cv_d = din("cv", [2, 64, 256])
    spT_d = din("spT", [128, NK, 16])
    gains_d = din("gains", [128, 8 * NK])
    pscale_d = din("pscale", [128, NK])
    bq_d = din("bq", [128, 20])
    bv_d = din("bv", [1, 256])
    sinks_d = din("sinks", [1, 32])
    rt_d = din("rt", [128, 128])
    icnt_d = din("icnt", [128, 4, 16])
    yT_d = dout("yT", [NPASS, NK, 128, T])
    kout_d = dout("kout", [2, 128, 192])
    vout_d = dout("vout", [3, 64, 256])
    pout_d = dout("pout", [128, NK, 32])

    with ExitStack() as es:
        sb = lambda n, s, dt: es.enter_context(nc.sbuf_tensor(n, s, dt))
        sem = lambda n: es.enter_context(nc.semaphore(n))
        PE = Eng(nc.tensor, sem("s_pe"))
        PE.is_pe = True
        ACT = Eng(nc.scalar, sem("s_act"))
        DVE = Eng(nc.vector, sem("s_dve"))
        POOL = Eng(nc.gpsimd, sem("s_pool"))
        SP = Eng(nc.sync, sem("s_sp"))
        pe, act, dve, pool = nc.tensor, nc.scalar, nc.vector, nc.gpsimd

        def tiles2(tensor, n):
            return [[Tl(tensor[:, k, TO[t]:TO[t] + TW[t]]) for t in range(2)] for k in range(n)]

        X_t = sb("X", [128, NK, T], F32)
        Y_t = sb("Y", [128, NK, T], F32)
        H_t = sb("H", [128, NK, T], BF16)
        U_t = sb("U", [128, G, T], BF16)
        X, Y, H, U = tiles2(X_t, NK), tiles2(Y_t, NK), tiles2(H_t, NK), tiles2(U_t, G)
        xsem = [sem(f"s_x{k}") for k in range(NK)]
        ysem = [sem(f"s_y{k}") for k in range(NK)]
        for k in range(NK):
            X[k][0].dsem = xsem[k]
            Y[k][0].dsem = ysem[k]
        NSL = 3
        slab_t = [sb(f"slab{i}", [128, 4096], BF16) for i in range(NSL)]
        slabs = [Tl(slab_t[i][:], sem(f"s_slab{i}")) for i in range(NSL)]
        KA_t = [sb(f"KA{j}", [128, NSLOT * 64], BF16) for j in range(2)]
        KB_t = [sb(f"KB{j}", [128, NSLOT * 64], BF16) for j in range(2)]
        KA = [Tl(KA_t[j][:], sem(f"s_ka{j}")) for j in range(2)]
        KB = [Tl(KB_t[j][:], sem(f"s_kb{j}")) for j in range(2)]
        V_t = sb("V", [64, NSLOT, 256], BF16)
        Vt = Tl(V_t[:], sem("s_v"))
        PT_t = [sb(f"PT{j}", [64, 3, 512], BF16) for j in range(2)]
        PT = [Tl(PT_t[j][:]) for j in range(2)]
        RD_t = sb("RD", [128, 512], F32)
        RD = Tl(RD_t[:])
        TC_t = sb("TC", [128, T], F32)
        TS_t = sb("TS", [128, T], F32)
        TC, TS = Tl(TC_t[:], sem("s_tc")), Tl(TS_t[:], sem("s_ts"))
        PPT_t = sb("PPT", [128, 2, T], BF16)
        PPT = Tl(PPT_t[:], sem("s_ppt"))
        SQ_t = [sb(f"SQ{i}", [128, 384], BF16) for i in range(4)]
        SQ = [Tl(SQ_t[i][:]) for i in range(4)]
        RS_t = [sb(f"RS{i}", [128, 384], F32) for i in range(2)]
        RS = [Tl(RS_t[i][:]) for i in range(2)]
        TMP_t = [sb(f"TMP{i}", [128, 400], F32) for i in range(8)]
        TMP = [Tl(TMP_t[i][:], sem(f"s_tmp{i}")) for i in range(8)]
        PH_t = sb("PH", [128, NK, 16], F32)
        PH = Tl(PH_t[:])
        SPH_t = sb("SPH", [128, NK, 16], F32)
        SPH = Tl(SPH_t[:], sem("s_sph"))
        PO_t = sb("PO", [128, NK, 32], F32)
        PO = Tl(PO_t[:], sem("s_po"))
        VO_t = sb("VO", [64, 3, 256], F32)
        VO = Tl(VO_t[:], sem("s_vo"))
        CONST_t = sb("CONST", [128, 8 * NK + NK + 20], F32)
        CONST = Tl(CONST_t[:], sem("s_const"))
        GN = lambda i, k: CONST_t[:, i * NK + k:i * NK + k + 1]
        PSC = lambda k: CONST_t[:, 8 * NK + k:8 * NK + k + 1]
        BQ = lambda c: CONST_t[:, 9 * NK + c:9 * NK + c + 1]
        RT_t = sb("RT", [128, 128], F32)
        RT = Tl(RT_t[:], sem("s_rt"))
        IC_t = sb("IC", [128, 4, 16], F32)
        IC = Tl(IC_t[:], sem("s_ic"))
        ROW_t = sb("ROW", [1, 256 + 32], F32)
        ROW = Tl(ROW_t[:], sem("s_row"))
        ONES_t = sb("ONES", [128, 128], BF16)
        ONES32_t = sb("ONES32", [1, 128], F32)
        ONES = Tl(ONES_t[:])
        PS_t = [es.enter_context(nc.psum_tensor(f"ps{i}", [128, 512], F32)) for i in range(8)]
        PS = [Tl(PS_t[i][:]) for i in range(8)]
        for t_ in PS:
            t_.psum = True

        def bc64(a):
            return bass.AP(a.tensor, a.offset, [list(x) for x in a.ap] + [[0, 64]])

        def chk(name):
            if stop == name:
                raise StopBuild()

        cnt = {"ps": 0, "sq": 0, "rs": 0, "tmp": 0, "slab": 0, "issued": 0}

        def nxt(pool_, key):
            i = cnt[key]
            cnt[key] += 1
            return pool_[i % len(pool_)]

        op(DVE, lambda: dve.memset(ONES_t[:], 1.0), wr=[ONES])
        op(DVE, lambda: dve.memset(ONES32_t[:], 1.0), wr=[ONES])
        op(DVE, lambda: dve.memset(PH_t[:], 0.0), wr=[PH])
        op(DVE, lambda: dve.memset(PO_t[:], 0.0), wr=[PO])
        dma(SP, CONST_t[:, 0:8 * NK], gains_d, wr=[CONST])
        dma(SP, CONST_t[:, 8 * NK:9 * NK], pscale_d, wr=[CONST])
        dma(SP, CONST_t[:, 9 * NK:9 * NK + 20], bq_d, wr=[CONST])
        dma(SP, RT_t[:], rt_d, wr=[RT])
        dma(SP, IC_t[:], icnt_d, wr=[IC])
        dma(SP, ROW_t[:, 0:256], bv_d, wr=[ROW])
        dma(SP, ROW_t[:, 256:], sinks_d, wr=[ROW])
        dma(SP, SPH_t[:], spT_d, wr=[SPH])
        op(ACT, lambda: act.activation(out=ROW_t[:, 256:], in_=ROW_t[:, 256:], func=AF.Exp), rd=[ROW], wr=[ROW])
        for j in range(2):
            dma(POOL, KA_t[j][:, 13 * 64:15 * 64], ckA_d[j], wr=[KA[j]])
            dma(POOL, KB_t[j][:, 13 * 64:15 * 64], ckB_d[j], wr=[KB[j]])
            dma(POOL, V_t[:, 13 + j, :], cv_d[j], wr=[Vt])

        chk("const")

        def issue_upto(n):
            while cnt["issued"] <= n and cnt["issued"] < NPASS * NSLAB:
                i = cnt["issued"]
                s = slabs[i % NSL]
                dma(POOL, s.ap.rearrange("p (a b) -> p a b", a=2), wst_d[i % NSLAB].rearrange("p (a b) -> p a b", a=2), wr=[s])
                cnt["issued"] += 1

        def next_slab():
            i = cnt["slab"]
            cnt["slab"] += 1
            issue_upto(i + NSL - 1)
            return slabs[i % NSL], slab_t[i % NSL]

        def mmgroup(out_ap, pairs, rd, ps, first_extra=None):
            def fn():
                ins = None
                n = len(pairs)
                if first_extra is not None:
                    ins = pe.matmul(out_ap, first_extra[0], first_extra[1], start=True, stop=False)
                for i, (l, r) in enumerate(pairs):
                    ins = pe.matmul(out_ap, l, r, start=(i == 0 and first_extra is None), stop=(i == n - 1))
                return ins
            op(PE, fn, rd=rd, wr=[ps])

        def stats(src, t, act_only=False):
            w = TW[t]
            ps = nxt(PS, "ps")
            for k in range(NK):
                sq = nxt(SQ, "sq")
                if k % 2 == 0 or act_only:
                    op(ACT, lambda: act.activation(out=sq.ap[:, 0:w], in_=src[k][t].ap, func=AF.Square),
                       rd=[src[k][t]], wr=[sq])
                else:
                    op(DVE, lambda: dve.tensor_tensor(out=sq.ap[:, 0:w], in0=src[k][t].ap, in1=src[k][t].ap, op=ALU.mult),
                       rd=[src[k][t]], wr=[sq])
                op(PE, lambda: pe.matmul(ps.ap[:, 0:w], ONES_t[:], sq.ap[:, 0:w], start=(k == 0), stop=(k == NK - 1)),
                   rd=[sq, ONES], wr=[ps])
            rs = nxt(RS, "rs")
            op(ACT, lambda: act.activation(out=rs.ap[:, 0:w], in_=ps.ap[:, 0:w], func=AF.Ln, bias=EPS, scale=1.0 / D),
               rd=[ps], wr=[rs])
            op(ACT, lambda: act.activation(out=rs.ap[:, 0:w], in_=rs.ap[:, 0:w], func=AF.Exp, scale=-0.5), rd=[rs], wr=[rs])
            return rs

        def norm_to_h(gi):
            for t in range(2):
                rs = stats(X, t, act_only=True)
                w = TW[t]
                for k in range(NK):
                    op(DVE, lambda: dve.scalar_tensor_tensor(out=H[k][t].ap, in0=X[k][t].ap, scalar=GN(gi, k),
                                                             in1=rs.ap[:, 0:w], op0=ALU.mult, op1=ALU.mult),
                       rd=[X[k][t], rs, CONST], wr=[H[k][t]])

        def post_norm_add(gi):
            rss = [stats(Y, t, act_only=True) for t in range(2)]
            for t in range(2):
                rs = rss[t]
                w = TW[t]
                for k in range(NK):
                    tmp = nxt(TMP, "tmp")
                    op(DVE, lambda: dve.scalar_tensor_tensor(out=tmp.ap[:, 0:w], in0=Y[k][t].ap, scalar=GN(gi, k),
                                                             in1=rs.ap[:, 0:w], op0=ALU.mult, op1=ALU.mult),
                       rd=[Y[k][t], rs, CONST], wr=[tmp])
                    op(POOL, lambda: pool.tensor_tensor(out=X[k][t].ap, in0=X[k][t].ap, in1=tmp.ap[:, 0:w], op=ALU.add),
                       rd=[X[k][t], tmp], wr=[X[k][t]])

        def acc_y(m, t, ps, first):
            w = TW[t]
            if first:
                op(ACT, lambda: act.copy(out=Y[m][t].ap, in_=ps.ap[:, 0:w]), rd=[ps], wr=[Y[m][t]])
            else:
                op(DVE, lambda: dve.tensor_tensor(out=Y[m][t].ap, in0=Y[m][t].ap, in1=ps.ap[:, 0:w], op=ALU.add),
                   rd=[ps, Y[m][t]], wr=[Y[m][t]])

        def ffn_and_ple(l, ps_i):
            dma(POOL, PPT_t[:], pT_d[l, ps_i].rearrange("k p t -> p k t"), wr=[PPT])
            norm_to_h(4 * l + 2)
            for grp in range(NGRP):
                for s in range(G // 2):
                    sl, slt = next_slab()
                    v = slt[:].rearrange("p (k c) -> p k c", k=16)
                    for jj in range(2):
                        j = 2 * s + jj
                        for t in range(2):
                            w = TW[t]
                            ps = nxt(PS, "ps")
                            mmgroup(ps.ap[:, 0:w], [(v[:, k, jj * 128:(jj + 1) * 128], H[k][t].ap) for k in range(NK)],
                                    [sl] + [H[k][t] for k in range(NK)], ps)
                            tmp = nxt(TMP, "tmp")
                            op(ACT, lambda: act.activation(out=tmp.ap[:, 0:w], in_=ps.ap[:, 0:w], func=AF.Relu),
                               rd=[ps], wr=[tmp])
                            op(DVE, lambda: dve.tensor_tensor(out=U[j][t].ap, in0=tmp.ap[:, 0:w], in1=tmp.ap[:, 0:w], op=ALU.mult),
                               rd=[tmp], wr=[U[j][t]])
                ncol = 4096 // G
                for s in range(2048 // ncol):
                    sl, slt = next_slab()
                    v = slt[:].rearrange("p (j c) -> p j c", j=G)
                    for mm in range(ncol // 128):
                        m = s * (ncol // 128) + mm
                        for t in range(2):
                            w = TW[t]
                            ps = nxt(PS, "ps")
                            mmgroup(ps.ap[:, 0:w], [(v[:, j, mm * 128:(mm + 1) * 128], U[j][t].ap) for j in range(G)],
                                    [sl] + [U[j][t] for j in range(G)], ps)
                            acc_y(m, t, ps, grp == 0)
            chk(f"ffnraw{l}")
            post_norm_add(4 * l + 3)
            chk(f"ffn{l}")
            for k in range(NK):
                for t in range(2):
                    op(ACT, lambda: act.copy(out=H[k][t].ap, in_=X[k][t].ap), rd=[X[k][t]], wr=[H[k][t]])
            psl, pslt = next_slab()
            pv = pslt[:].rearrange("p (k c) -> p k c", k=2)
            for m in range(NK):
                for t in range(2):
                    w = TW[t]
                    ps2 = nxt(PS, "ps")
                    mmgroup(ps2.ap[:, 0:w], [(pv[:, kk, m * 128:(m + 1) * 128], PPT_t[:, kk, TO[t]:TO[t] + w]) for kk in range(2)],
                            [psl, PPT], ps2)
                    if (m + t) % 2 == 0:
                        op(ACT, lambda: act.copy(out=Y[m][t].ap, in_=ps2.ap[:, 0:w]), rd=[ps2], wr=[Y[m][t]])
                    else:
                        op(DVE, lambda: dve.tensor_copy(out=Y[m][t].ap, in_=ps2.ap[:, 0:w]), rd=[ps2], wr=[Y[m][t]])
            for s in range(8):
                sl, slt = next_slab()
                v = slt[:].rearrange("p (k c) -> p k c", k=16)
                for mm in range(2):
                    m = 2 * s + mm
                    for t in range(2):
                        w = TW[t]
                        ps = nxt(PS, "ps")
                        mmgroup(ps.ap[:, 0:w], [(v[:, k, mm * 128:(mm + 1) * 128], H[k][t].ap) for k in range(NK)],
                                [sl] + [H[k][t] for k in range(NK)], ps)
                        gt = nxt(TMP, "tmp")
                        op(ACT, lambda: act.activation(out=gt.ap[:, 0:w], in_=ps.ap[:, 0:w], func=AF.Sigmoid), rd=[ps], wr=[gt])
                        op(DVE, lambda: dve.tensor_tensor(out=gt.ap[:, 0:w], in0=gt.ap[:, 0:w], in1=Y[m][t].ap, op=ALU.mult),
                           rd=[gt, Y[m][t]], wr=[gt])
                        op(DVE, lambda: dve.tensor_tensor(out=X[m][t].ap, in0=X[m][t].ap, in1=gt.ap[:, 0:w], op=ALU.add),
                           rd=[X[m][t], gt], wr=[X[m][t]])

        pend = []

        def flush_rope():
            while pend:
                pend.pop(0)()

        def rope_chunk(ps, bias_col, t, dst_ap, dst_tile, keep32=None):
            w, c0 = TW[t], TO[t]
            qb = nxt(TMP, "tmp")
            op(ACT, lambda: act.activation(out=qb.ap[:, 0:w], in_=ps.ap[:, 0:w], func=AF.Identity, bias=BQ(bias_col), scale=1.0),
               rd=[ps, CONST], wr=[qb])
            prev = list(pend)
            del pend[:]
            pend.append(lambda: rope_tail(qb, t, dst_ap, dst_tile, keep32))
            for f_ in prev:
                f_()

        def rope_tail(qb, t, dst_ap, dst_tile, keep32):
            w, c0 = TW[t], TO[t]
            ps2 = nxt(PS, "ps")
            op(PE, lambda: pe.matmul(ps2.ap[:, 0:w], RT_t[:], qb.ap[:, 0:w], start=True, stop=True), rd=[RT, qb], wr=[ps2])
            t2 = nxt(TMP, "tmp")
            op(DVE, lambda: dve.tensor_tensor(out=t2.ap[:, 0:w], in0=ps2.ap[:, 0:w], in1=TS_t[:, c0:c0 + w], op=ALU.mult),
               rd=[ps2, TS], wr=[t2])
            op(DVE, lambda: dve.tensor_tensor(out=qb.ap[:, 0:w], in0=qb.ap[:, 0:w], in1=TC_t[:, c0:c0 + w], op=ALU.mult),
               rd=[qb, TC], wr=[qb])
            if keep32 is not None:
                op(DVE, lambda: dve.tensor_tensor(out=keep32.ap[:, 0:w], in0=qb.ap[:, 0:w], in1=t2.ap[:, 0:w], op=ALU.add),
                   rd=[qb, t2], wr=[keep32])
                op(ACT, lambda: act.copy(out=dst_ap, in_=keep32.ap[:, 0:w]), rd=[keep32], wr=[dst_tile])
            else:
                op(DVE, lambda: dve.tensor_tensor(out=dst_ap, in0=qb.ap[:, 0:w], in1=t2.ap[:, 0:w], op=ALU.add),
                   rd=[qb, t2], wr=[dst_tile])

        def attention_layer(ps_i):
            chk("xload")
            norm_to_h(0)
            chk("norm")
            for kb in range(2):
                sl, slt = next_slab()
                v = slt[:].rearrange("p (k c) -> p k c", k=16)
                Kt_t, Kt = (KA_t, KA) if kb == 0 else (KB_t, KB)
                for j in range(2):
                    for t in range(2):
                        w = TW[t]
                        ps = nxt(PS, "ps")
                        mmgroup(ps.ap[:, 0:w], [(v[:, k, j * 128:(j + 1) * 128], H[k][t].ap) for k in range(NK)],
                                [sl] + [H[k][t] for k in range(NK)], ps)
                        keep = None
                        if kb == 0 and ps_i == 2 and t == 1:
                            keep = nxt(TMP, "tmp")
                        rope_chunk(ps, 16 + 2 * kb + j, t, Kt_t[j][:, 128 + TO[t]:128 + TO[t] + w], Kt[j], keep32=keep)
                        if keep is not None:
                            flush_rope()
                            dma(SP, kout_d[j], keep.ap[:, 128:320], rd=[keep])
            flush_rope()
            chk("k")
            sl, slt = next_slab()
            v = slt[:].rearrange("p (k c) -> p k c", k=16)
            for i in range(NCH):
                t = 0 if i < 6 else 1
                lc = 64 * i - TO[t]
                ps = nxt(PS, "ps")
                mmgroup(ps.ap[0:64, 0:256], [(H_t[:, k, 64 * i:64 * i + 64], v[:, k, :]) for k in range(NK)],
                        [sl, ROW] + [H[k][t] for k in range(NK)], ps,
                        first_extra=(ONES32_t[0:1, 0:64], ROW_t[0:1, 0:256]))
                op(ACT, lambda: act.copy(out=V_t[:, 2 + i, :], in_=ps.ap[0:64, 0:256]), rd=[ps], wr=[Vt])
                if ps_i == 2 and i >= 8:
                    op(ACT, lambda: act.copy(out=VO_t[:, i - 8, :], in_=ps.ap[0:64, 0:256]), rd=[ps], wr=[VO])
            if ps_i == 2:
                dma(SP, vout_d.rearrange("c p f -> p c f"), VO_t[:], rd=[VO])
            chk("v")
            for g in range(4):
                for s in range(2):
                    sl, slt = next_slab()
                    v = slt[:].rearrange("p (k c) -> p k c", k=16)
                    for jj in range(2):
                        c = 2 * s + jj
                        for t in range(2):
                            w = TW[t]
                            ps = nxt(PS, "ps")
                            mmgroup(ps.ap[:, 0:w], [(v[:, k, jj * 128:(jj + 1) * 128], H[k][t].ap) for k in range(NK)],
                                    [sl] + [H[k][t] for k in range(NK)], ps)
                            rope_chunk(ps, 4 * g + c, t, U[c][t].ap, U[c][t])
                flush_rope()
                chk("q0")
                Ke_t, Ke = (KA_t[g // 2], KA[g // 2]) if g % 2 == 0 else (KB_t[g // 2], KB[g // 2])
                Ko_t, Ko = (KB_t[g // 2], KB[g // 2]) if g % 2 == 0 else (KA_t[g // 2], KA[g // 2])
                def scores(i):
                    t = 0 if i < 6 else 1
                    qs = slice(64 * i, 64 * i + 64)
                    if ps_i == 2 and i == NCH - 1:
                        keys = [13, 14, 12]
                    else:
                        keys = [s_ for s_ in (i, i + 1, i + 2) if NCH * ps_i + s_ - 2 >= 0]
                    qrd = [U[c][t] for c in range(4)]
                    pt_t, ptl = PT_t[i % 2], PT[i % 2]
                    pts = [(jx, s_) for jx, s_ in enumerate(keys)]
                    for j0 in (0, 2):
                        grp_ = [p_ for p_ in pts if j0 <= p_[0] < j0 + 2]
                        if not grp_:
                            continue
                        n_ = len(grp_)
                        bE = nxt(PS, "ps")
                        bO = nxt(PS, "ps")
                        def fn():
                            ins = None
                            for jx, s_ in grp_:
                                ks = slice(64 * s_, 64 * s_ + 64)
                                cs_ = slice(256 * (jx - j0), 256 * (jx - j0) + 256)
                                pe.matmul(bE.ap[0:64, cs_], Ke_t[0:64, ks], U_t[0:64, 0:4, qs], start=True, stop=True)
                                ins = pe.matmul(bO.ap[0:64, cs_], Ko_t[64:128, ks], U_t[64:128, 0:4, qs], start=True, stop=True)
                            return ins
                        op(PE, fn, rd=[Ke, Ko] + qrd, wr=[bE, bO])
                        v3 = lambda ap_: ap_.rearrange("p (a b) -> p a b", a=n_)
                        op(ACT, lambda: act.activation(out=pt_t[:, j0:j0 + n_, 0:256], in_=v3(bE.ap[0:64, 0:256 * n_]), func=AF.Exp, scale=0.125),
                           rd=[bE], wr=[ptl])
                        op(ACT, lambda: act.activation(out=pt_t[:, j0:j0 + n_, 256:512], in_=v3(bO.ap[0:64, 0:256 * n_]), func=AF.Exp, scale=0.125),
                           rd=[bO], wr=[ptl])
                    return (t, qs, pt_t, ptl, pts)

                def finish(info):
                    t, qs, pt_t, ptl, pts = info
                    pdp = nxt(PS, "ps")
                    def fn_dp():
                        pe.matmul(pdp.ap[0:64, 0:256], ONES32_t[0:1, 0:64], bc64(ROW_t[0:1, 256 + 8 * g:256 + 8 * g + 4]), start=True, stop=False)
                        pe.matmul(pdp.ap[64:128, 0:256], ONES32_t[0:1, 0:64], bc64(ROW_t[0:1, 256 + 8 * g + 4:256 + 8 * g + 8]),
                                  start=True, stop=False, tile_position=(0, 64))
                        ins = None
                        for jx, s_ in pts:
                            last = jx == len(pts) - 1
                            pe.matmul(pdp.ap[0:64, 0:256], ONES_t[0:64, 0:64], pt_t[:, jx, 0:256], start=False, stop=last)
                            pe.matmul(pdp.ap[64:128, 0:256], ONES_t[0:64, 0:64], pt_t[:, jx, 256:512], start=False, stop=last,
                                      tile_position=(0, 64))
                        for jx, s_ in pts:
                            first, last = jx == 0, jx == len(pts) - 1
                            pe.matmul(pdp.ap[0:64, 256:512], V_t[:, s_, 64 * g:64 * g + 64], pt_t[:, jx, 0:256], start=first, stop=last)
                            ins = pe.matmul(pdp.ap[64:128, 256:512], V_t[:, s_, 64 * g:64 * g + 64], pt_t[:, jx, 256:512], start=first, stop=last,
                                            tile_position=(0, 64))
                        return ins
                    op(PE, fn_dp, rd=[ROW, ONES, ptl, Vt], wr=[pdp])
                    op(ACT, lambda: act.activation(out=RD_t[:, 0:256], in_=pdp.ap[:, 0:256], func=AF.Ln), rd=[pdp], wr=[RD])
                    op(ACT, lambda: act.activation(out=RD_t[:, 0:256], in_=RD_t[:, 0:256], func=AF.Exp, scale=-1.0), rd=[RD], wr=[RD])
                    r3 = lambda ap_: ap_.rearrange("p (a b) -> p a b", a=4)
                    op(DVE, lambda: dve.tensor_tensor(out=U_t[:, 4:8, qs], in0=r3(pdp.ap[:, 256:512]), in1=r3(RD_t[:, 0:256]), op=ALU.mult),
                       rd=[pdp, RD], wr=[U[4 + c][t] for c in range(4)])

                info = scores(0)
                for i in range(NCH):
                    nxt_info = scores(i + 1) if i + 1 < NCH else None
                    finish(info)
                    info = nxt_info
                for s in range(2):
                    sl, slt = next_slab()
                    v = slt[:].rearrange("p (k c) -> p k c", k=4)
                    for mm in range(8):
                        m = 8 * s + mm
                        for t in range(2):
                            w = TW[t]
                            ps = nxt(PS, "ps")
                            mmgroup(ps.ap[:, 0:w], [(v[:, c, mm * 128:(mm + 1) * 128], U[4 + c][t].ap) for c in range(4)],
                                    [sl] + [U[4 + c][t] for c in range(4)], ps)
                            acc_y(m, t, ps, g == 0)
            chk("attn")
            if ps_i < NPASS - 1:
                for j in range(2):
                    op(ACT, lambda: act.copy(out=KA_t[j][:, 0:128], in_=KA_t[j][:, 11 * 64:13 * 64]), rd=[KA[j]], wr=[KA[j]])
                    op(ACT, lambda: act.copy(out=KB_t[j][:, 0:128], in_=KB_t[j][:, 11 * 64:13 * 64]), rd=[KB[j]], wr=[KB[j]])
                op(ACT, lambda: act.copy(out=V_t[:, 0:2, :], in_=V_t[:, 11:13, :]), rd=[Vt], wr=[Vt])
            post_norm_add(1)
            chk("mix0")

        def pool_layer(ps_i):
            for t in range(2):
                rs = stats(X, t, act_only=True)
                w, c0 = TW[t], TO[t]
                for k in range(NK):
                    grp = k // 4
                    L = grp + 1
                    A = nxt(TMP, "tmp")
                    op(DVE, lambda: dve.tensor_copy(out=A.ap[:, 0:16], in_=PH_t[:, k, :]), rd=[PH], wr=[A])
                    op(DVE, lambda: dve.scalar_tensor_tensor(out=A.ap[:, 16:16 + w], in0=X[k][t].ap, scalar=GN(4, k),
                                                             in1=rs.ap[:, 0:w], op0=ALU.mult, op1=ALU.mult),
                       rd=[X[k][t], rs, CONST], wr=[A])
                    op(DVE, lambda: dve.tensor_copy(out=PH_t[:, k, :], in_=A.ap[:, w:w + 16]), rd=[A], wr=[PH])
                    if ps_i == 2 and t == 1:
                        op(DVE, lambda: dve.tensor_copy(out=PO_t[:, k, 0:16], in_=A.ap[:, 16 + 240:16 + 256]), rd=[A], wr=[PO])
                        op(DVE, lambda: dve.tensor_copy(out=PO_t[:, k, 16:32], in_=A.ap[:, 16 + 304:16 + 320]), rd=[A], wr=[PO])

                    def window(Abuf, n, out_ap, special):
                        src = Abuf
                        sh = 1
                        for step in range(L):
                            dst = nxt(TMP, "tmp")
                            lo = 2 * sh
                            op(DVE, lambda: dve.tensor_tensor(out=dst.ap[:, lo:16 + n], in0=src.ap[:, lo:16 + n],
                                                              in1=src.ap[:, lo - sh:16 + n - sh], op=ALU.add),
                               rd=[src], wr=[dst])
                            src = dst
                            sh *= 2
                        op(DVE, lambda: dve.scalar_tensor_tensor(out=out_ap, in0=src.ap[:, 16:16 + n], scalar=1.0 / POOLW[grp],
                                                                 in1=Abuf.ap[:, 16:16 + n], op0=ALU.mult, op1=ALU.subtract),
                           rd=[src, Abuf], wr=[H[k][t]])
                        if special:
                            tt = nxt(TMP, "tmp")
                            op(DVE, lambda: dve.tensor_tensor(out=tt.ap[:, 0:16], in0=src.ap[:, 16:32], in1=IC_t[:, grp, :], op=ALU.mult),
                               rd=[src, IC], wr=[tt])
                            op(DVE, lambda: dve.tensor_tensor(out=H_t[:, k, c0:c0 + 16], in0=tt.ap[:, 0:16], in1=Abuf.ap[:, 16:32], op=ALU.subtract),
                               rd=[tt, Abuf], wr=[H[k][t]])

                    window(A, w, H[k][t].ap, ps_i == 0 and t == 0)
                    if ps_i == 2 and t == 1:
                        As = nxt(TMP, "tmp")
                        op(DVE, lambda: dve.tensor_copy(out=As.ap[:, 0:16], in_=SPH_t[:, k, :]), rd=[SPH], wr=[As])
                        op(DVE, lambda: dve.tensor_copy(out=As.ap[:, 16:80], in_=A.ap[:, 16 + 256:16 + 320]), rd=[A], wr=[As])
                        window(As, 64, H_t[:, k, 640:704], False)
            for s in range(2):
                sl, slt = next_slab()
                v = slt[:].rearrange("p (g k c) -> p g k c", g=2, k=4)
                for gi in range(2):
                    g = 2 * s + gi
                    for mm in range(4):
                        m = 4 * g + mm
                        for t in range(2):
                            w = TW[t]
                            ps = nxt(PS, "ps")
                            mmgroup(ps.ap[:, 0:w], [(v[:, gi, k, mm * 128:(mm + 1) * 128], H[4 * g + k][t].ap) for k in range(4)],
                                    [sl] + [H[4 * g + k][t] for k in range(4)], ps)
                            op(ACT, lambda: act.activation(out=Y[m][t].ap, in_=ps.ap[:, 0:w], func=AF.Copy, scale=PSC(m)),
                               rd=[ps, CONST], wr=[Y[m][t]])
            if ps_i == 2:
                dma(SP, pout_d, PO_t[:], rd=[PO])
            post_norm_add(5)

        out_evs = []
        try:
          for ps_i in range(NPASS):
            for k in range(NK):
                dma(SP, X_t[:, k, :], xT_d[ps_i, k], wr=[X[k][0], X[k][1]], ev_tile=X[k][0])
            dma(SP, TC_t[:], tabC_d[ps_i], wr=[TC])
            dma(SP, TS_t[:], tabS_d[ps_i], wr=[TS])
            attention_layer(ps_i)
            chk(f"mix0_p{ps_i}")
            ffn_and_ple(0, ps_i)
            chk("l0")
            pool_layer(ps_i)
            chk("pool")
            ffn_and_ple(1, ps_i)
            chk(f"l1_p{ps_i}")
            for k in range(NK):
                ev = dma(SP, yT_d[ps_i, k], X_t[:, k, :], rd=[X[k][0], X[k][1]], ev_tile=X[k][0])
                out_evs.append(ev)
            X, Y = Y, X
            X_t, Y_t = Y_t, X_t
        except StopBuild:
            for E_ in (PE, ACT, DVE, POOL):
                if E_.n:
                    nc.sync.wait_ge(E_.sem, E_.n)
            for t_ in slabs:
                if t_.dn:
                    nc.sync.wait_ge(t_.dsem, 16 * t_.dn)
        for t_ in (VO, PO) + tuple(TMP):
            if t_.dn:
                out_evs.append((t_.dsem, 16 * t_.dn, None))
        for (s_, v_, _) in out_evs:
            nc.sync.wait_ge(s_, v_)
    return nc


_CACHE = {}


def host_inputs(x_prompt, x_sample, cache_k, cache_v, state_pool, p_prompt, p_sample,
                norm_mix_pre, norm_mix_post, norm_ffn_pre, norm_ffn_post,
                w_qkv, b_qkv, w_o, sinks, w_pool, pool_scale,
                w_ffn_up, w_ffn_down, w_ple_proj, w_ple_gate, cores=range(8)):
    f32 = lambda a: np.asarray(a, dtype=np.float32)
    x_prompt, x_sample, p_prompt, p_sample = f32(x_prompt), f32(x_sample), f32(p_prompt), f32(p_sample)
    cache_k, cache_v, state_pool = f32(cache_k), f32(cache_v), f32(state_pool)
    wst = build_wstream(f32(w_qkv), f32(w_o), f32(w_pool), f32(w_ffn_up), f32(w_ffn_down), f32(w_ple_proj), f32(w_ple_gate))
    pos = np.arange(NPASS * T, dtype=np.float32)
    inv = (np.float32(500000.0) ** (-np.arange(0, 16, 2, dtype=np.float32) / np.float32(16))).astype(np.float32)
    ang = pos[:, None] * inv[None, :]
    cs, sn = np.cos(ang).astype(np.float32), np.sin(ang).astype(np.float32)
    tabC = np.ones((NPASS * T, 128), np.float32)
    tabS = np.zeros((NPASS * T, 128), np.float32)
    for hb in (0, 64):
        tabC[:, hb:hb + 8] = cs
        tabC[:, hb + 8:hb + 16] = cs
        tabS[:, hb:hb + 8] = -sn
        tabS[:, hb + 8:hb + 16] = sn
    tabC = np.ascontiguousarray(tabC.reshape(NPASS, T, 128).transpose(0, 2, 1))
    tabS = np.ascontiguousarray(tabS.reshape(NPASS, T, 128).transpose(0, 2, 1))
    rt = np.zeros((128, 128), np.float32)
    for hb in (0, 64):
        for d in range(8):
            rt[hb + d + 8, hb + d] = 1.0
            rt[hb + d, hb + d + 8] = 1.0
    icnt = np.ones((128, 4, 16), np.float32)
    for gi, w in enumerate(POOLW):
        icnt[:, gi, :] = 1.0 / np.minimum(np.arange(16) + 1, w).astype(np.float32)
    vec = lambda v: np.ascontiguousarray(f32(v).reshape(NK, 128).T)
    gains = np.concatenate([vec(norm_mix_pre[0]), vec(norm_mix_post[0]), vec(norm_ffn_pre[0]), vec(norm_ffn_post[0]),
                            vec(norm_mix_pre[1]), vec(norm_mix_post[1]), vec(norm_ffn_pre[1]), vec(norm_ffn_post[1])], axis=1)
    pscale = vec(pool_scale[0])
    b = f32(b_qkv)[0]
    bk = b[2048:2304]
    bkswap = np.concatenate([bk[64:128], bk[0:64], bk[192:256], bk[128:192]])
    bq = np.ascontiguousarray(np.concatenate([b[0:2048], bk, bkswap]).reshape(20, 128).T)
    bv = np.ascontiguousarray(b[2304:2560].reshape(1, 256))
    sk = f32(sinks)[0]
    order = [8 * g + 2 * i + par for g in range(4) for par in range(2) for i in range(4)]
    sinks_exp = np.ascontiguousarray(sk[order].reshape(1, 32))

    in_maps = []
    for c in cores:
        seq = np.concatenate([x_prompt[c], x_sample[c]], axis=0)
        xT = np.ascontiguousarray(seq.reshape(NPASS, T, NK, 128).transpose(0, 2, 3, 1))
        pp = np.concatenate([p_prompt[:, c], p_sample[:, c]], axis=1)
        pT = np.ascontiguousarray(pp.reshape(2, NPASS, T, 2, 128).transpose(0, 1, 3, 4, 2))
        ck = cache_k[0, c].reshape(128, 256)
        ckA = np.ascontiguousarray(ck.T.reshape(2, 128, 128))
        ckB = np.ascontiguousarray(np.concatenate([ckA[:, 64:128], ckA[:, 0:64]], axis=1))
        cv = np.ascontiguousarray(cache_v[0, c].reshape(2, 64, 256))
        sp = state_pool[0, c]
        spT = np.zeros((128, NK, 16), np.float32)
        spT[:, :, 1:16] = sp.reshape(15, NK, 128).transpose(2, 1, 0)
        in_maps.append(dict(xT=xT, pT=pT, wst=wst, tabC=tabC, tabS=tabS, ckA=ckA, ckB=ckB, cv=cv, spT=spT,
                            gains=gains, pscale=pscale, bq=bq, bv=bv, sinks=sinks_exp, rt=rt, icnt=icnt))
    return in_maps


def assemble(R):
    y_prompt = np.zeros((8, 2048, D), np.float32)
    y_sample = np.zeros((8, 64, D), np.float32)
    nkp = np.zeros((1, 8, 128, 4, 64), np.float32)
    nvp = np.zeros((1, 8, 128, 4, 64), np.float32)
    npp = np.zeros((1, 8, 15, D), np.float32)
    nks = np.zeros((1, 8, 64, 4, 64), np.float32)
    nvs = np.zeros((1, 8, 64, 4, 64), np.float32)
    nps = np.zeros((1, 8, 15, D), np.float32)
    for c in range(8):
        yT = np.asarray(R[c]["yT"])
        seq = yT.transpose(0, 3, 1, 2).reshape(NPASS * T, D)
        y_prompt[c] = seq[:2048]
        y_sample[c] = seq[2048:]
        ko = np.asarray(R[c]["kout"]).reshape(256, 192).T
        nkp[0, c] = ko[:128].reshape(128, 4, 64)
        nks[0, c] = ko[128:].reshape(64, 4, 64)
        vo = np.asarray(R[c]["vout"]).reshape(192, 256)
        nvp[0, c] = vo[:128].reshape(128, 4, 64)
        nvs[0, c] = vo[128:].reshape(64, 4, 64)
        po = np.asarray(R[c]["pout"])
        npp[0, c] = po[:, :, 1:16].transpose(2, 1, 0).reshape(15, D)
        nps[0, c] = po[:, :, 17:32].transpose(2, 1, 0).reshape(15, D)
    return (y_prompt, y_sample, nkp, nvp, npp, nks, nvs, nps)


def kernel(**inputs):
    if "nc" not in _CACHE:
        _CACHE["nc"] = build_program()
    nc = _CACHE["nc"]
    in_maps = host_inputs(**inputs)
    res = run_bass_kernel_spmd(nc, in_maps, core_ids=list(range(8)))
    return assemble(res.results)
```
